# Optimizing a Trainium2 kernel written in Bass

```python
import jax, jax.numpy as jnp
from jax import lax
import numpy as np

D_MODEL = 1024
BATCH = 16
SEQ = 4096
DEPTH = 1

MIX_WIDTH = D_MODEL
SB_HEADS = 8
SB_HEAD_DIM = 64
SB_WIDTH = SB_HEADS * SB_HEAD_DIM
SB_SCALE = SB_HEAD_DIM ** -0.5
MLA_HEADS = 4
QK_NOPE = 128
QK_ROPE = 64
V_DIM = 128
Q_LORA = 256
KV_LORA = 128
MLA_WIDTH = MLA_HEADS * V_DIM
MLA_SCALE = (QK_NOPE + QK_ROPE) ** -0.5
ROPE_BASE = 10000.0
IN_COLS = 3 * SB_WIDTH + Q_LORA + KV_LORA + QK_ROPE
D_FF = 4 * D_MODEL
BLOCK_Q = 128
NORM_EPS = 1e-6

kernel_name = 'hymba_stickbreaking_mla_sqrelu_block'


def rmsnorm(x, g):
    xf = x.astype(jnp.float32)
    y = xf * lax.rsqrt(jnp.mean(jnp.square(xf), axis=-1, keepdims=True) + NORM_EPS)
    return (y * g.astype(jnp.float32)).astype(x.dtype)


def rope_tables(positions):
    half = QK_ROPE // 2
    inv_freq = ROPE_BASE ** (-jnp.arange(half, dtype=jnp.float32) / half)
    ang = positions.astype(jnp.float32)[..., None] * inv_freq
    return jnp.cos(ang), jnp.sin(ang)


def apply_rope(x, cos, sin):
    half = QK_ROPE // 2
    xf = x.astype(jnp.float32)
    x1, x2 = xf[..., :half], xf[..., half:]
    out = jnp.concatenate([x1 * cos - x2 * sin, x2 * cos + x1 * sin], axis=-1)
    return out.astype(x.dtype)


def stick_breaking_block(q_blk, k_pre, v_pre, t0):
    n_q, n_k = q_blk.shape[1], k_pre.shape[1]
    z = jnp.einsum('bqhd,bkhd->bhqk', q_blk, k_pre).astype(jnp.float32) * SB_SCALE
    t_idx = t0 + jnp.arange(n_q)
    s_idx = jnp.arange(n_k)
    strict = s_idx[None, :] < t_idx[:, None]
    sp = jnp.where(strict, jax.nn.softplus(z), 0.0)
    excl = lax.cumsum(sp, axis=3, reverse=True) - sp
    log_a = jax.nn.log_sigmoid(z) - excl
    a = jnp.where(strict, jnp.exp(log_a), 0.0)
    return jnp.einsum('bhqk,bkhd->bqhd', a.astype(v_pre.dtype), v_pre)


def mla_block(qn_blk, qr_blk, kn_pre, kr_pre, v_pre, t0):
    n_q, n_k = qn_blk.shape[1], kn_pre.shape[1]
    s = (jnp.einsum('bqhd,bkhd->bhqk', qn_blk, kn_pre)
         + jnp.einsum('bqhr,bkr->bhqk', qr_blk, kr_pre)).astype(jnp.float32) * MLA_SCALE
    t_idx = t0 + jnp.arange(n_q)
    s_idx = jnp.arange(n_k)
    causal = s_idx[None, :] <= t_idx[:, None]
    s = jnp.where(causal, s, jnp.finfo(jnp.float32).min)
    p = jax.nn.softmax(s, axis=-1)
    return jnp.einsum('bhqk,bkhd->bqhd', p.astype(v_pre.dtype), v_pre)


def setup_inputs(seed: int = 0) -> dict:
    key = jax.random.key(seed)
    ks = jax.random.split(key, 16)

    def w(k, shape, fan_in):
        return jax.random.normal(k, shape, jnp.float32) * fan_in ** -0.5

    def g(k, shape):
        return 1.0 + 0.01 * jax.random.normal(k, shape, jnp.float32)

    return {
        'x': jax.random.normal(ks[0], (BATCH, SEQ, D_MODEL), jnp.float32),
        'positions': jnp.broadcast_to(jnp.arange(SEQ, dtype=jnp.int32), (BATCH, SEQ)),
        'attn_norm_g': g(ks[1], (DEPTH, D_MODEL)),
        'w_in': w(ks[2], (DEPTH, D_MODEL, IN_COLS), D_MODEL),
        'q_a_norm_g': g(ks[3], (DEPTH, Q_LORA)),
        'w_q_b': w(ks[4], (DEPTH, Q_LORA, MLA_HEADS * (QK_NOPE + QK_ROPE)), Q_LORA),
        'kv_a_norm_g': g(ks[5], (DEPTH, KV_LORA)),
        'w_kv_b': w(ks[6], (DEPTH, KV_LORA, MLA_HEADS * (QK_NOPE + V_DIM)), KV_LORA),
        'sb_out_norm_g': g(ks[7], (DEPTH, SB_WIDTH)),
        'mla_out_norm_g': g(ks[8], (DEPTH, MLA_WIDTH)),
        'w_o': w(ks[9], (DEPTH, MIX_WIDTH, D_MODEL), MIX_WIDTH),
        'mlp_norm_g': g(ks[10], (DEPTH, D_MODEL)),
        'w_up': w(ks[11], (DEPTH, D_MODEL, D_FF), D_MODEL),
        'w_down': w(ks[12], (DEPTH, D_FF, D_MODEL), D_FF),
        'final_norm_g': g(ks[13], (D_MODEL,)),
    }


def reference(x, positions, attn_norm_g, w_in, q_a_norm_g, w_q_b, kv_a_norm_g, w_kv_b,
              sb_out_norm_g, mla_out_norm_g, w_o, mlp_norm_g, w_up, w_down, final_norm_g):
    bsz, seq = x.shape[0], x.shape[1]
    n_blocks = seq // BLOCK_Q
    cos, sin = rope_tables(positions)
    splits = [SB_WIDTH, 2 * SB_WIDTH, 3 * SB_WIDTH, 3 * SB_WIDTH + Q_LORA,
              3 * SB_WIDTH + Q_LORA + KV_LORA]
    h = x
    for l in range(DEPTH):
        u = rmsnorm(h, attn_norm_g[l])
        proj = jnp.einsum('bsd,de->bse', u, w_in[l])
        q_sb, k_sb, v_sb, c_q, c_kv, k_rope = jnp.split(proj, splits, axis=-1)
        q_sb = q_sb.reshape(bsz, seq, SB_HEADS, SB_HEAD_DIM)
        k_sb = k_sb.reshape(bsz, seq, SB_HEADS, SB_HEAD_DIM)
        v_sb = v_sb.reshape(bsz, seq, SB_HEADS, SB_HEAD_DIM)
        q_m = jnp.einsum('bsr,re->bse', rmsnorm(c_q, q_a_norm_g[l]), w_q_b[l])
        q_m = q_m.reshape(bsz, seq, MLA_HEADS, QK_NOPE + QK_ROPE)
        q_nope = q_m[..., :QK_NOPE]
        q_rope = apply_rope(q_m[..., QK_NOPE:], cos[:, :, None, :], sin[:, :, None, :])
        kv = jnp.einsum('bsr,re->bse', rmsnorm(c_kv, kv_a_norm_g[l]), w_kv_b[l])
        kv = kv.reshape(bsz, seq, MLA_HEADS, QK_NOPE + V_DIM)
        k_nope, v_m = kv[..., :QK_NOPE], kv[..., QK_NOPE:]
        k_rope = apply_rope(k_rope, cos, sin)
        sb_outs, mla_outs = [], []
        for i in range(n_blocks):
            t0 = i * BLOCK_Q
            t1 = t0 + BLOCK_Q
            sb_outs.append(stick_breaking_block(q_sb[:, t0:t1], k_sb[:, :t1], v_sb[:, :t1], t0))
            mla_outs.append(mla_block(q_nope[:, t0:t1], q_rope[:, t0:t1], k_nope[:, :t1],
                                      k_rope[:, :t1], v_m[:, :t1], t0))
        o_sb = jnp.concatenate(sb_outs, axis=1).reshape(bsz, seq, SB_WIDTH)
        o_mla = jnp.concatenate(mla_outs, axis=1).reshape(bsz, seq, MLA_WIDTH)
        mixed = jnp.concatenate([rmsnorm(o_sb, sb_out_norm_g[l]),
                                 rmsnorm(o_mla, mla_out_norm_g[l])], axis=-1)
        h = h + jnp.einsum('bse,ed->bsd', mixed, w_o[l])
        v = rmsnorm(h, mlp_norm_g[l])
        hid = jnp.square(jax.nn.relu(jnp.einsum('bsd,df->bsf', v, w_up[l])))
        h = h + jnp.einsum('bsf,fd->bsd', hid, w_down[l])
    return rmsnorm(h, final_norm_g)
```

```python
import math
from contextlib import ExitStack

import ml_dtypes
import numpy as np

import concourse.bass as bass
import concourse.mybir as mybir
from concourse.bass_utils import run_bass_kernel_spmd

F32 = mybir.dt.float32
BF16 = mybir.dt.bfloat16
I32 = mybir.dt.int32
AF = mybir.ActivationFunctionType
ALU = mybir.AluOpType

N_CORES = 8
ARENA_BYTES = 200704
EPS = 1e-6
TWO_PI = 2.0 * math.pi
C1 = 6.28125
C2 = TWO_PI - C1
NEG = -30000.0


class Buf:
    __slots__ = ("name", "w", "rs")

    def __init__(self, name):
        self.name = name
        self.w = None
        self.rs = []


class Trk:
    ENGS = ("pe", "act", "dve", "pool", "sp")

    def __init__(self):
        self.ins = {e: [] for e in self.ENGS}
        self.known = {e: {} for e in self.ENGS}
        self.pending = {e: [] for e in self.ENGS}
        self.epoch = 0
        self.dma_cnt = {}
        self.bufs = []

    def buf(self, name):
        b = Buf(name)
        self.bufs.append(b)
        return b

    def bufs_n(self, name, n):
        return [self.buf(f"{name}{i}") for i in range(n)]

    def _collect(self, eng, reads, writes):
        deps = set()
        for b in reads:
            if b.w is not None:
                deps.add(b.w)
        for b in writes:
            if b.w is not None:
                deps.add(b.w)
            deps.update(b.rs)
        idx = len(self.ins[eng])
        best = {}
        for t in deps:
            if t[0] == "e":
                if t[1] == eng:
                    if eng in ("pe", "sp") or idx - t[2] > 2:
                        continue
                key = ("e", t[1])
            else:
                key = ("d", t[1])
            if key not in best or best[key] < t[2]:
                best[key] = t[2]
        waits = list(self.pending[eng])
        self.pending[eng] = []
        kn = self.known[eng]
        for key, v in best.items():
            if key[0] == "e" and key[1] == eng:
                waits.append(("e", eng, v))
                self.ins[eng][v]["signal"] = True
                continue
            if kn.get(key, -1) >= v:
                continue
            kn[key] = v
            waits.append((key[0], key[1], v))
            if key[0] == "e":
                self.ins[key[1]][v]["signal"] = True
        return waits

    def op(self, eng, fn, reads=(), writes=()):
        waits = self._collect(eng, reads, writes)
        idx = len(self.ins[eng])
        tok = ("e", eng, idx)
        self.ins[eng].append(dict(fn=fn, waits=waits, signal=False, epoch=self.epoch, dma=None))
        ws = set(id(b) for b in writes)
        for b in writes:
            b.w = tok
            b.rs = []
        for b in reads:
            if id(b) not in ws:
                b.rs.append(tok)
        return tok

    def dma(self, fn, key, reads=(), writes=()):
        waits = self._collect("sp", reads, writes)
        cnt = self.dma_cnt.get(key, 0) + 1
        self.dma_cnt[key] = cnt
        tok = ("d", key, cnt)
        self.ins["sp"].append(dict(fn=fn, waits=waits, signal=False, epoch=self.epoch, dma=key))
        ws = set(id(b) for b in writes)
        for b in writes:
            b.w = tok
            b.rs = []
        for b in reads:
            if id(b) not in ws:
                b.rs.append(tok)
        return tok

    def barrier(self, new_epoch=True):
        for e in self.ENGS:
            kn = self.known[e]
            for s in self.ENGS:
                if s == e or s == "sp" or not self.ins[s]:
                    continue
                v = len(self.ins[s]) - 1
                if kn.get(("e", s), -1) < v:
                    kn[("e", s)] = v
                    self.pending[e].append(("e", s, v))
                    self.ins[s][v]["signal"] = True
            for key, cnt in self.dma_cnt.items():
                if kn.get(("d", key), 0) < cnt:
                    kn[("d", key)] = cnt
                    self.pending[e].append(("d", key, cnt))
        for b in self.bufs:
            b.w = None
            b.rs = []
        if new_epoch:
            self.epoch += 1

    def emit(self, nc, es):
        esem = {}
        for e in self.ENGS:
            if e == "sp":
                continue
            for ep in sorted(set(i["epoch"] for i in self.ins[e])):
                esem[(e, ep)] = es.enter_context(nc.semaphore(f"s_{e}_{ep}"))
        dsem = {k: es.enter_context(nc.semaphore("d_" + "_".join(str(x) for x in k))) for k in self.dma_cnt}
        rank = {}
        for e in self.ENGS:
            cnt = {}
            r = []
            for i in self.ins[e]:
                if i["signal"]:
                    cnt[i["epoch"]] = cnt.get(i["epoch"], 0) + 1
                r.append(cnt.get(i["epoch"], 0))
            rank[e] = r
        final_waits = [("d", k, c) for k, c in self.dma_cnt.items()]
        block = es.enter_context(nc.Block())

        def run(e_name, eng):
            for i in self.ins[e_name]:
                for t in i["waits"]:
                    if t[0] == "e":
                        eng.wait_ge(esem[(t[1], self.ins[t[1]][t[2]]["epoch"])], rank[t[1]][t[2]])
                    else:
                        eng.wait_ge(dsem[t[1]], 16 * t[2])
                r = i["fn"](eng)
                if i["signal"]:
                    r.then_inc(esem[(e_name, i["epoch"])], 1)
                if i["dma"] is not None:
                    r.then_inc(dsem[i["dma"]], 16)
            for t in self.pending[e_name]:
                if t[0] == "e":
                    eng.wait_ge(esem[(t[1], self.ins[t[1]][t[2]]["epoch"])], rank[t[1]][t[2]])
                else:
                    eng.wait_ge(dsem[t[1]], 16 * t[2])
            if e_name == "sp":
                for t in final_waits:
                    eng.wait_ge(dsem[t[1]], 16 * t[2])

        @block.tensor
        def _(eng):
            run("pe", eng)

        @block.scalar
        def _(eng):
            run("act", eng)

        @block.vector
        def _(eng):
            run("dve", eng)

        @block.gpsimd
        def _(eng):
            run("pool", eng)

        @block.sync
        def _(eng):
            run("sp", eng)


def build(NSEQ, NQB, dbg=False):
    S = NQB * 512
    NT = NSEQ * S
    NBLK = NSEQ * NQB
    nc = bass.Bass("TRN2", target_bir_lowering=False)
    x = nc.dram_tensor("x", [NT, 1024], F32, kind="ExternalInput").ap()
    pos = nc.dram_tensor("pos", [NSEQ, S], I32, kind="ExternalInput").ap()
    w_in = nc.dram_tensor("w_in", [1024, 1984], F32, kind="ExternalInput").ap()
    w_qb = nc.dram_tensor("w_qb", [256, 768], F32, kind="ExternalInput").ap()
    w_kvb = nc.dram_tensor("w_kvb", [128, 1024], F32, kind="ExternalInput").ap()
    w_o = nc.dram_tensor("w_o", [1024, 1024], F32, kind="ExternalInput").ap()
    w_up = nc.dram_tensor("w_up", [1024, 4096], F32, kind="ExternalInput").ap()
    w_dn = nc.dram_tensor("w_dn", [4096, 1024], F32, kind="ExternalInput").ap()
    gpk_d = nc.dram_tensor("gpk", [128, 32], F32, kind="ExternalInput").ap()
    gfin_d = nc.dram_tensor("gfin", [1, 1024], F32, kind="ExternalInput").ap()
    cst_d = nc.dram_tensor("cstb", [128, 1152], BF16, kind="ExternalInput").ap()
    out = nc.dram_tensor("out", [NT, 1024], F32, kind="ExternalOutput").ap()
    mixs = nc.dram_tensor("mixs", [NBLK, 128, 8, 512], BF16,
                          kind="ExternalOutput" if dbg else "Internal").ap()

    T = Trk()
    with ExitStack() as es:
        arena = es.enter_context(nc.sbuf_tensor("arena", [128, ARENA_BYTES // 2], BF16))
        cst = es.enter_context(nc.sbuf_tensor("cst", [128, 1152], BF16))
        gpk = es.enter_context(nc.sbuf_tensor("gpkt", [128, 32], F32))
        epst = es.enter_context(nc.sbuf_tensor("epst", [128, 1], F32))
        stat = es.enter_context(nc.sbuf_tensor("stat", [128, 16], F32))
        ps = es.enter_context(nc.psum_tensor("ps", [128, 8, 512], F32))
        psB = T.bufs_n("ps", 8)
        cstB = T.buf("cst")
        statB = {}

        ident = cst[:, 0:128]
        triU = cst[:, 128:256]
        ones = cst[:, 256:384]
        maskSB = cst[:, 384:512]
        maskML = cst[:, 512:640]
        Dmat = cst[:, 640:768]
        Dprev = cst[:, 768:896]
        ShiftM = cst[:, 896:1024]
        maskCp = cst[:, 1024:1152]

        class Arena:
            def __init__(self, off=0):
                self.off = off

            def take(self, shape, dt):
                n = int(np.prod(shape[1:]))
                sz = 2 if dt == BF16 else 4
                nb = n * sz
                nb = (nb + 63) // 64 * 64
                assert self.off + nb <= ARENA_BYTES, (self.off, nb, shape)
                v = arena[0:shape[0], self.off // 2:self.off // 2 + n * sz // 2]
                self.off += nb
                if dt != BF16:
                    v = v.bitcast(dt)
                if len(shape) == 3:
                    v = v.rearrange("p (a b) -> p a b", a=shape[1])
                return v

        def mm(out_ap, lhsT, rhs, start, stop, reads, writes):
            T.op("pe", lambda e: e.matmul(out_ap, lhsT=lhsT, rhs=rhs, start=start, stop=stop),
                 reads=reads, writes=writes)

        def act(out_ap, in_ap, func, reads, writes, scale=1.0, bias=None, accum=None):
            kw = {}
            if bias is not None:
                kw["bias"] = bias
            if accum is not None:
                kw["accum_out"] = accum
            T.op("act", lambda e: e.activation(out=out_ap, in_=in_ap, func=func, scale=scale, **kw),
                 reads=reads, writes=writes)

        def tcopy(eng, out_ap, in_ap, reads, writes):
            T.op(eng, lambda e: e.tensor_copy(out=out_ap, in_=in_ap), reads=reads, writes=writes)

        def tt(eng, out_ap, in0, in1, op, reads, writes):
            T.op(eng, lambda e: e.tensor_tensor(out=out_ap, in0=in0, in1=in1, op=op), reads=reads, writes=writes)

        def ts(eng, out_ap, in0, s1, s2, op0, op1, reads, writes):
            if s2 is None:
                T.op(eng, lambda e: e.tensor_scalar(out=out_ap, in0=in0, scalar1=s1, scalar2=None, op0=op0),
                     reads=reads, writes=writes)
            else:
                T.op(eng, lambda e: e.tensor_scalar(out=out_ap, in0=in0, scalar1=s1, scalar2=s2, op0=op0, op1=op1),
                     reads=reads, writes=writes)

        def stt(eng, out_ap, in0, scalar, in1, op0, op1, reads, writes):
            T.op(eng, lambda e: e.scalar_tensor_tensor(out=out_ap, in0=in0, scalar=scalar, in1=in1, op0=op0, op1=op1),
                 reads=reads, writes=writes)

        def dma(out_ap, in_ap, key, reads, writes):
            T.dma(lambda e: e.dma_start(out=out_ap, in_=in_ap), key, reads=reads, writes=writes)

        def rstd_small(ssq_col, ln_col, rs_col, inv_n, bS, bL, bR):
            act(ln_col, ssq_col, AF.Ln, reads=[bS, epsB], writes=[bL], scale=inv_n, bias=epst[:, 0:1])
            act(rs_col, ln_col, AF.Exp, reads=[bL], writes=[bR], scale=-0.5)

        def rstd_big(ps_ap, psb, rs, rsB, inv_n, P=128):
            act(rs, ps_ap, AF.Ln, reads=[psb, epsB], writes=[rsB], scale=inv_n, bias=epst[0:P, 0:1])
            act(rs, rs, AF.Exp, reads=[], writes=[rsB], scale=-0.5)

        gpkB = T.buf("gpk")
        epsB = T.buf("eps")
        dma(cst[:, :], cst_d[:, :], ("c", 0), [], [cstB])
        dma(gpk[:, :], gpk_d[:, :], ("c", 1), [], [gpkB])
        T.op("pool", lambda e: e.memset(epst[:, :], EPS), writes=[epsB])

        A0 = Arena(0)
        wall = A0.take([128, 8, 1984], BF16)
        wsw = A0.take([128, 8, 64], BF16)
        wqb = A0.take([128, 2, 768], BF16)
        wqbs = A0.take([128, 2, 256], BF16)
        wkvb = A0.take([128, 1024], BF16)
        wkvv = A0.take([128, 512], BF16)
        W_END = A0.off
        wallB, wswB, wqbB, wqbsB, wkvbB, wkvvB = (T.buf(n) for n in ("wall", "wsw", "wqb", "wqbs", "wkvb", "wkvv"))

        def front_alloc(A):
            d = {}
            d["xt"] = [A.take([128, 1024], F32) for _ in range(2)]
            d["utok"] = A.take([128, 1024], BF16)
            d["uT"] = A.take([128, 8, 512], BF16)
            return d

        xtB = T.bufs_n("xt", 2)
        utokB = T.buf("utok")
        junkB = T.buf("junk")
        uTB = T.buf("uT")
        ssqB = T.bufs_n("ssq", 4)
        lnB = T.bufs_n("ln", 4)
        rsB = T.bufs_n("rs", 4)

        def load_weights_attn(fr):
            xt = fr["xt"]
            engs = ["dve", "pool"]
            n = 0
            for k in range(8):
                for hh, (c0, c1) in enumerate(((0, 1024), (1024, 1984))):
                    w = c1 - c0
                    dma(xt[hh][:, 0:w], w_in[k * 128:(k + 1) * 128, c0:c1], ("x", hh), [], [xtB[hh]])
                    tcopy(engs[n % 2], wall[:, k, c0:c1], xt[hh][:, 0:w], [xtB[hh]], [wallB])
                    n += 1
            for k in range(2):
                dma(xt[k][:, 0:768], w_qb[k * 128:(k + 1) * 128, :], ("x", k), [], [xtB[k]])
                tcopy(engs[k], wqb[:, k, :], xt[k][:, 0:768], [xtB[k]], [wqbB])
            dma(xt[0][:, :], w_kvb[:, :], ("x", 0), [], [xtB[0]])
            tcopy("dve", wkvb[:, :], xt[0][:, :], [xtB[0]], [wkvbB])
            ts("dve", wsw[:, :, 0:32], wall[:, :, 1952:1984], -1.0, None, ALU.mult, None, [wallB], [wswB])
            tcopy("dve", wsw[:, :, 32:64], wall[:, :, 1920:1952], [wallB], [wswB])
            for h in range(4):
                b = h * 192 + 128
                ts("dve", wqbs[:, :, h * 64:h * 64 + 32], wqb[:, :, b + 32:b + 64], -1.0, None, ALU.mult, None,
                   [wqbB], [wqbsB])
                tcopy("dve", wqbs[:, :, h * 64 + 32:h * 64 + 64], wqb[:, :, b:b + 32], [wqbB], [wqbsB])
                tcopy("pool", wkvv[:, h * 128:(h + 1) * 128], wkvb[:, h * 256 + 128:h * 256 + 256], [wkvbB], [wkvvB])

        def frontend(fr, t0, gcol0, tbank, use_act=False):
            xt, utok, uT = fr["xt"], fr["utok"], fr["uT"]
            for c in range(4):
                xb, xB = xt[c % 2], xtB[c % 2]
                dma(xb[:, :], x[t0 + c * 128:t0 + (c + 1) * 128, :], ("x", c % 2), [], [xB])
                act(utok[:, :], xb[:, :], AF.Square, reads=[xB], writes=[utokB, ssqB[c]], accum=stat[:, c:c + 1])
                rstd_small(stat[:, c:c + 1], stat[:, 4 + c:5 + c], stat[:, 8 + c:9 + c], 1.0 / 1024,
                           ssqB[c], lnB[c], rsB[c])
                yield
                ts("dve", utok[:, :], xb[:, :], stat[:, 8 + c:9 + c], None, ALU.mult, None, [xB, rsB[c]], [utokB])
                bank = tbank()
                tpv = ps[:, bank, :].bitcast(BF16)
                for k in range(8):
                    T.op("pe", (lambda o, i: (lambda e: e.transpose(o, i, ident)))(tpv[:, k * 128:(k + 1) * 128],
                                                                                     utok[:, k * 128:(k + 1) * 128]),
                         reads=[utokB, cstB], writes=[psB[bank]])
                    if k == 3:
                        yield
                yield
                for k in range(8):
                    if use_act and k % 2 == 1:
                        act(uT[:, k, c * 128:(c + 1) * 128], tpv[:, k * 128:(k + 1) * 128], AF.Copy,
                            reads=[psB[bank], gpkB], writes=[uTB], scale=gpk[:, gcol0 + k:gcol0 + k + 1])
                    else:
                        ts("dve", uT[:, k, c * 128:(c + 1) * 128], tpv[:, k * 128:(k + 1) * 128],
                           gpk[:, gcol0 + k:gcol0 + k + 1], None, ALU.mult, None, [psB[bank], gpkB], [uTB])
                yield

        def pull(g, n):
            if g is None:
                return
            for _ in range(n):
                try:
                    next(g)
                except StopIteration:
                    return

        def drain(g):
            if g is None:
                return
            for _ in g:
                pass

        def chain8(out_ap, bank, lhs_fn, rhs_fn, extra_reads):
            for k in range(8):
                mm(out_ap, lhs_fn(k), rhs_fn(k), k == 0, k == 7, reads=extra_reads, writes=[psB[bank]])

        def chain8g(out_ap, bank, lhs_fn, rhs_fn, extra_reads):
            for k in range(8):
                mm(out_ap, lhs_fn(k), rhs_fn(k), k == 0, k == 7, reads=extra_reads, writes=[psB[bank]])
                if k == 3:
                    yield

        rr = [0]

        def next_bank(nb=4):
            b = rr[0] % nb
            rr[0] += 1
            return b

        def phase_sb(seq):
            A = Arena(W_END)
            kT = A.take([128, 4, S], BF16)
            DVc = A.take([128, NQB * 4, 512], BF16)
            fr = front_alloc(A)
            uT = fr["uT"]
            qT = [A.take([128, 4, 512], BF16) for _ in range(2)]
            Vcur = [A.take([128, 4, 512], BF16) for _ in range(2)]
            Vlast = [A.take([128, 512], BF16) for _ in range(3)]
            Eb = [A.take([128, 2, 512], F32) for _ in range(2)]
            spb = [A.take([128, 2, 512], BF16) for _ in range(3)]
            Wb = [A.take([128, 2, 512], BF16) for _ in range(2)]
            Sb = [A.take([128, 2, 512], BF16) for _ in range(2)]
            oT = A.take([128, 4, 512], F32)
            sq = A.take([128, 4, 512], BF16)
            rstd = A.take([128, 512], F32)
            mixo = [A.take([128, 4, 512], BF16) for _ in range(2)]
            kTB = [T.bufs_n(f"kT{m}_", NQB) for m in range(4)]
            DVB = T.bufs_n("DV", NQB * 4)
            qTB = [T.bufs_n(f"qT{p}_", 4) for p in range(2)]
            VcB = [T.bufs_n(f"Vc{p}_", 4) for p in range(2)]
            VlB = T.bufs_n("Vl", 3)
            EB = T.bufs_n("E", 2)
            SPB = T.bufs_n("sp", 3)
            WB = T.bufs_n("W", 2)
            SB_ = T.bufs_n("S", 2)
            oTB = T.bufs_n("oT", 4)
            sqB, rstdB = T.buf("sq"), T.buf("rstd")
            mixoB = T.bufs_n("mixo", 2)
            if seq == 0:
                load_weights_attn(fr)
            chain_ctr = [0]
            PB = 7

            def prep(j):
                par = j % 2
                t0 = seq * S + j * 512
                yield from frontend(fr, t0, 0, lambda: PB)
                b = PB
                for m in range(4):
                    yield from chain8g(ps[:, b, :], b, lambda k: wall[:, k, m * 128:(m + 1) * 128], lambda k: uT[:, k, :],
                           [wallB, uTB])
                    tcopy("dve", qT[par][:, m, :], ps[:, b, :], [psB[b]], [qTB[par][m]])
                    yield
                for m in range(4):
                    yield from chain8g(ps[:, b, :], b, lambda k: wall[:, k, 512 + m * 128:512 + (m + 1) * 128],
                           lambda k: uT[:, k, :], [wallB, uTB])
                    tcopy("dve", kT[:, m, j * 512:(j + 1) * 512], ps[:, b, :], [psB[b]], [kTB[m][j]])
                    yield
                for c in range(4):
                    yield from chain8g(ps[:, b, :], b, lambda k: uT[:, k, c * 128:(c + 1) * 128], lambda k: wall[:, k, 1024:1536],
                           [wallB, uTB])
                    tcopy("dve", Vcur[par][:, c, :], ps[:, b, :], [psB[b]], [VcB[par][c]])
                    yield
                tcopy("dve", Vlast[j % 3][:, :], Vcur[par][:, 3, :], [VcB[par][3]], [VlB[j % 3]])
                for c in range(4):
                    if c > 0:
                        prev, prevB = Vcur[par][:, c - 1, :], VcB[par][c - 1]
                    elif j > 0:
                        prev, prevB = Vlast[(j - 1) % 3][:, :], VlB[(j - 1) % 3]
                    else:
                        prev, prevB = None, None
                    mm(ps[:, b, :], Dmat, Vcur[par][:, c, :], True, prev is None, [VcB[par][c], cstB], [psB[b]])
                    if prev is not None:
                        mm(ps[:, b, :], Dprev, prev, False, True, [prevB, cstB], [psB[b]])
                    tcopy("dve", DVc[:, j * 4 + c, :], ps[:, b, :], [psB[b]], [DVB[j * 4 + c]])
                    yield

            def run_steps(j, gnext):
                par = j % 2
                steps = []
                for m in range(4):
                    ch = chain_ctr[0]
                    chain_ctr[0] += 1
                    for kt in range(4 * j + 3, -1, -1):
                        dg = kt - 4 * j
                        steps.append(dict(m=m, kt=kt, dg=dg, c0=128 * dg if dg >= 0 else 0,
                                          first=(kt == 4 * j + 3), last=(kt == 0), ch=ch))
                n = len(steps)

                def stZ(i):
                    s = steps[i]
                    zb = 2 * (i % 2)
                    m, kt, c0 = s["m"], s["kt"], s["c0"]
                    for e in range(2):
                        pr = slice(e * 64, (e + 1) * 64)
                        mm(ps[:, zb + e, c0:512], kT[pr, m, kt * 128:(kt + 1) * 128], qT[par][pr, m, c0:512],
                           True, s["dg"] < 0, [kTB[m][kt // 4], qTB[par][m]], [psB[zb + e]])
                        if s["dg"] >= 0:
                            mm(ps[:, zb + e, c0:c0 + 128], ident, maskSB, False, True, [cstB], [psB[zb + e]])

                def stE(i):
                    s = steps[i]
                    zb = 2 * (i % 2)
                    c0 = s["c0"]
                    act(Eb[i % 2][:, :, c0:512], ps[:, zb:zb + 2, c0:512], AF.Exp, reads=[psB[zb], psB[zb + 1]],
                        writes=[EB[i % 2]], scale=0.125)

                def stSP(i):
                    s = steps[i]
                    c0 = s["c0"]
                    act(spb[i % 3][:, :, c0:512], Eb[i % 2][:, :, c0:512], AF.Ln, reads=[EB[i % 2]],
                        writes=[SPB[i % 3]], bias=1.0)

                def stC(i):
                    s = steps[i]
                    c0 = s["c0"]
                    sp = s["ch"] % 2
                    for e in range(2):
                        mm(ps[:, 4 + e, c0:512], triU, spb[i % 3][:, e, c0:512], True, s["first"] and s["dg"] < 0,
                           [SPB[i % 3], cstB], [psB[4 + e]])
                        if not s["first"]:
                            mm(ps[:, 4 + e, c0:512], ones, Sb[sp][:, e, c0:512], False, s["dg"] < 0,
                               [SB_[sp], cstB], [psB[4 + e]])
                        if s["dg"] >= 0:
                            mm(ps[:, 4 + e, c0:c0 + 128], ident, maskCp, False, True, [cstB], [psB[4 + e]])

                def stS(i):
                    s = steps[i]
                    if s["last"]:
                        return
                    c0 = s["c0"]
                    sp = s["ch"] % 2
                    if s["first"]:
                        T.op("dve", lambda e: e.memset(Sb[sp][:, :, :], 0.0), writes=[SB_[sp]])
                    tt("dve", Sb[sp][:, :, c0:512], Sb[sp][:, :, c0:512], spb[i % 3][:, :, c0:512], ALU.add,
                       [SPB[i % 3]], [SB_[sp]])

                def stW(i):
                    s = steps[i]
                    c0 = s["c0"]
                    act(Wb[i % 2][:, :, c0:512], ps[:, 4:6, c0:512], AF.Exp, reads=[psB[4], psB[5]],
                        writes=[WB[i % 2]], scale=-1.0)

                def stAV(i):
                    s = steps[i]
                    m, kt, c0 = s["m"], s["kt"], s["c0"]
                    ob = 6
                    for e in range(2):
                        h = 2 * m + e
                        pr = slice(e * 64, (e + 1) * 64)
                        mm(ps[pr, ob, c0:512], DVc[:, kt, h * 64:(h + 1) * 64], Wb[i % 2][:, e, c0:512],
                           s["first"], s["last"] and s["dg"] < 0, [DVB[kt], WB[i % 2]], [psB[ob]])
                        if s["dg"] >= 0:
                            dg = s["dg"]
                            if dg > 0:
                                pv, pvB = Vcur[par][:, dg - 1, :], VcB[par][dg - 1]
                            elif j > 0:
                                pv, pvB = Vlast[(j - 1) % 3][:, :], VlB[(j - 1) % 3]
                            else:
                                pv, pvB = None, None
                            mm(ps[pr, ob, c0:c0 + 128], Vcur[par][:, dg, h * 64:(h + 1) * 64], ShiftM,
                               False, s["last"] and pv is None, [VcB[par][dg], cstB], [psB[ob]])
                            if pv is not None:
                                mm(ps[pr, ob, c0:c0 + 128], pv[:, h * 64:(h + 1) * 64], Dprev,
                                   False, s["last"], [pvB, cstB], [psB[ob]])
                    if s["last"]:
                        tcopy("dve", oT[:, m, :], ps[:, ob, :], [psB[ob]], [oTB[m]])

                for it in range(-3, n):
                    if 0 <= it + 3 < n:
                        stZ(it + 3)
                    if 0 <= it + 2 < n:
                        stE(it + 2)
                    if 0 <= it + 1 < n:
                        stSP(it + 1)
                    if 0 <= it < n:
                        stW(it)
                        stAV(it)
                    if 0 <= it + 1 < n:
                        stC(it + 1)
                        stS(it + 1)
                    pull(gnext, -(-46 // n))

            def outnorm(j):
                par = j % 2
                blk = seq * NQB + j
                tt("pool", sq[:, :, :], oT[:, :, :], oT[:, :, :], ALU.mult, oTB, [sqB])
                b = PB
                for m in range(4):
                    mm(ps[:, b, :], ones, sq[:, m, :], m == 0, m == 3, [sqB, cstB], [psB[b]])
                rstd_big(ps[:, b, :], psB[b], rstd[:, :], rstdB, 1.0 / 512)
                for m in range(4):
                    stt("dve", mixo[par][:, m, :], oT[:, m, :], gpk[:, 11 + m:12 + m], rstd[:, :], ALU.mult, ALU.mult,
                        [oTB[m], rstdB, gpkB], [mixoB[par]])
                dma(mixs[blk, :, 0:4, :], mixo[par][:, :, :], ("mix", par), [mixoB[par]], [])

            drain(prep(0))
            for j in range(NQB):
                gnext = prep(j + 1) if j + 1 < NQB else None
                run_steps(j, gnext)
                outnorm(j)
                drain(gnext)

        def phase_mla(seq):
            A = Arena(W_END)
            knT = A.take([128, 4, S], BF16)
            krT = A.take([64, S], BF16)
            Vm = A.take([128, NQB * 4, 512], BF16)
            fr = front_alloc(A)
            uT = fr["uT"]
            cT = A.take([128, 3, 512], F32)
            sq = A.take([128, 4, 512], BF16)
            rstd = [A.take([128, 512], F32) for _ in range(2)]
            cn = A.take([128, 3, 512], BF16)
            ang = A.take([64, 512], F32)
            posi = ang.bitcast(I32)
            a2 = A.take([64, 512], F32)
            kf = A.take([64, 512], F32)
            ki = kf.bitcast(I32)
            tab = [A.take([64, 512], F32) for _ in range(2)]
            ro = A.take([64, 512], F32)
            rs_ = A.take([64, 512], F32)
            qnT = [A.take([128, 4, 512], BF16) for _ in range(2)]
            qrT = [A.take([64, 4, 512], BF16) for _ in range(2)]
            Pb = [A.take([128, 512], BF16) for _ in range(3)]
            Pacc = [A.take([128, 512], F32) for _ in range(2)]
            Pbf = A.take([128, 512], BF16)
            omT = A.take([128, 4, 512], F32)
            mixo = A.take([128, 4, 512], BF16)
            knTB = [T.bufs_n(f"knT{h}_", NQB) for h in range(4)]
            krTB = T.bufs_n("krT", NQB)
            VmB = T.bufs_n("Vm", NQB * 4)
            cTB = T.bufs_n("cT", 3)
            sqB = T.buf("sq")
            rstdB = T.bufs_n("rstd", 2)
            cnB = T.bufs_n("cn", 3)
            angB, a2B, kfB = (T.buf(nm) for nm in ("ang", "a2", "kf"))
            tabB = T.bufs_n("tab", 2)
            roB, rsB_ = T.buf("ro"), T.buf("rs_")
            qnTB = [T.bufs_n(f"qnT{p}_", 4) for p in range(2)]
            qrTB = [T.bufs_n(f"qrT{p}_", 4) for p in range(2)]
            PB_ = T.bufs_n("P", 3)
            PaccB = T.bufs_n("Pacc", 2)
            PbfB = T.buf("Pbf")
            omTB = T.bufs_n("omT", 4)
            mixoB = T.buf("mixo")
            chain_ctr = [0]
            nbk = lambda: next_bank(4)
            evn = [0]

            def evac(out_ap, in_ap, reads, writes):
                evn[0] += 1
                if evn[0] % 2:
                    act(out_ap, in_ap, AF.Copy, reads=reads, writes=writes)
                else:
                    tcopy("dve", out_ap, in_ap, reads, writes)

            invf = gpk[0:64, 27:28]
            sin_t, cos_t = tab[0], tab[1]

            def rope(out_ap, outB, o_ap, oB, s_ap, sB):
                tt("dve", ro[:, :], o_ap, cos_t[:, :], ALU.mult, [oB, tabB[1]], [roB])
                tt("dve", rs_[:, :], s_ap, sin_t[:, :], ALU.mult, [sB, tabB[0]], [rsB_])
                tt("pool", out_ap, ro[:, :], rs_[:, :], ALU.add, [roB, rsB_], [outB])

            def prep(j):
                par = j % 2
                t0 = seq * S + j * 512
                yield from frontend(fr, t0, 0, nbk, use_act=True)
                dma(posi[:, :], pos[seq:seq + 1, j * 512:(j + 1) * 512].partition_broadcast(64), ("pos", 0), [], [angB])
                tcopy("dve", ang[:, :], posi[:, :], [], [angB])
                ts("pool", ang[:, :], ang[:, :], invf, None, ALU.mult, None, [gpkB], [angB])
                for ti, shift in enumerate((0.0, math.pi / 2)):
                    ts("pool", a2[:, :], ang[:, :], shift, None, ALU.add, None, [angB], [a2B])
                    ts("pool", kf[:, :], a2[:, :], 1.0 / TWO_PI, None, ALU.mult, None, [a2B], [kfB])
                    tcopy("dve", ki[:, :], kf[:, :], [], [kfB])
                    tcopy("dve", kf[:, :], ki[:, :], [], [kfB])
                    stt("dve", a2[:, :], kf[:, :], -C1, a2[:, :], ALU.mult, ALU.add, [kfB], [a2B])
                    stt("dve", a2[:, :], kf[:, :], -C2, a2[:, :], ALU.mult, ALU.add, [kfB], [a2B])
                    ts("pool", a2[:, :], a2[:, :], -3.141592, 3.141592, ALU.max, ALU.min, [], [a2B])
                    act(tab[ti][:, :], a2[:, :], AF.Sin, reads=[a2B], writes=[tabB[ti]])
                    yield
                for idx, col0 in enumerate((1536, 1664, 1792)):
                    b = nbk()
                    yield from chain8g(ps[:, b, :], b, lambda k: wall[:, k, col0:col0 + 128], lambda k: uT[:, k, :], [wallB, uTB])
                    evac(cT[:, idx, :], ps[:, b, :], [psB[b]], [cTB[idx]])
                    yield
                tt("pool", sq[:, 0:3, :], cT[:, :, :], cT[:, :, :], ALU.mult, cTB, [sqB])
                bq = nbk()
                mm(ps[:, bq, :], ones, sq[:, 0, :], True, False, [sqB, cstB], [psB[bq]])
                mm(ps[:, bq, :], ones, sq[:, 1, :], False, True, [sqB, cstB], [psB[bq]])
                bk = nbk()
                mm(ps[:, bk, :], ones, sq[:, 2, :], True, True, [sqB, cstB], [psB[bk]])
                rstd_big(ps[:, bq, :], psB[bq], rstd[0][:, :], rstdB[0], 1.0 / 256)
                rstd_big(ps[:, bk, :], psB[bk], rstd[1][:, :], rstdB[1], 1.0 / 128)
                yield
                for idx in range(3):
                    r = 0 if idx < 2 else 1
                    stt("dve", cn[:, idx, :], cT[:, idx, :], gpk[:, 8 + idx:9 + idx], rstd[r][:, :], ALU.mult, ALU.mult,
                        [cTB[idx], rstdB[r], gpkB], [cnB[idx]])
                yield
                b1 = nbk()
                yield from chain8g(ps[0:64, b1, :], b1, lambda k: wall[:, k, 1920:1984], lambda k: uT[:, k, :], [wallB, uTB])
                b2 = nbk()
                yield from chain8g(ps[0:64, b2, :], b2, lambda k: wsw[:, k, :], lambda k: uT[:, k, :], [wswB, uTB])
                rope(krT[:, j * 512:(j + 1) * 512], krTB[j], ps[0:64, b1, :], psB[b1], ps[0:64, b2, :], psB[b2])
                yield
                for h in range(4):
                    b = nbk()
                    for k in range(2):
                        mm(ps[:, b, :], wqb[:, k, h * 192:h * 192 + 128], cn[:, k, :], k == 0, k == 1,
                           [wqbB, cnB[k]], [psB[b]])
                    evac(qnT[par][:, h, :], ps[:, b, :], [psB[b]], [qnTB[par][h]])
                    b1 = nbk()
                    for k in range(2):
                        mm(ps[0:64, b1, :], wqb[:, k, h * 192 + 128:h * 192 + 192], cn[:, k, :], k == 0, k == 1,
                           [wqbB, cnB[k]], [psB[b1]])
                    b2 = nbk()
                    for k in range(2):
                        mm(ps[0:64, b2, :], wqbs[:, k, h * 64:(h + 1) * 64], cn[:, k, :], k == 0, k == 1,
                           [wqbsB, cnB[k]], [psB[b2]])
                    rope(qrT[par][:, h, :], qrTB[par][h], ps[0:64, b1, :], psB[b1], ps[0:64, b2, :], psB[b2])
                    yield
                for h in range(4):
                    b = nbk()
                    mm(ps[:, b, :], wkvb[:, h * 256:h * 256 + 128], cn[:, 2, :], True, True, [wkvbB, cnB[2]], [psB[b]])
                    evac(knT[:, h, j * 512:(j + 1) * 512], ps[:, b, :], [psB[b]], [knTB[h][j]])
                    yield
                for c in range(4):
                    b = nbk()
                    mm(ps[:, b, :], cn[:, 2, c * 128:(c + 1) * 128], wkvv[:, :], True, True, [wkvvB, cnB[2]], [psB[b]])
                    evac(Vm[:, j * 4 + c, :], ps[:, b, :], [psB[b]], [VmB[j * 4 + c]])
                    yield

            def run_steps(j, gnext):
                par = j % 2
                steps = []
                for h in range(4):
                    ch = chain_ctr[0]
                    chain_ctr[0] += 1
                    for kt in range(4 * j + 3, -1, -1):
                        dg = kt - 4 * j
                        steps.append(dict(h=h, kt=kt, dg=dg, c0=128 * dg if dg >= 0 else 0,
                                          first=(kt == 4 * j + 3), last=(kt == 0), ch=ch))
                n = len(steps)
                sc = 192.0 ** -0.5

                def mZ(i):
                    s = steps[i]
                    zb = 4 + (i % 2)
                    h, kt, c0 = s["h"], s["kt"], s["c0"]
                    mm(ps[:, zb, c0:512], knT[:, h, kt * 128:(kt + 1) * 128], qnT[par][:, h, c0:512], True, False,
                       [knTB[h][kt // 4], qnTB[par][h]], [psB[zb]])
                    mm(ps[:, zb, c0:512], krT[:, kt * 128:(kt + 1) * 128], qrT[par][:, h, c0:512], False, s["dg"] < 0,
                       [krTB[kt // 4], qrTB[par][h]], [psB[zb]])
                    if s["dg"] >= 0:
                        mm(ps[:, zb, c0:c0 + 128], ident, maskML, False, True, [cstB], [psB[zb]])

                def mP(i):
                    s = steps[i]
                    zb = 4 + (i % 2)
                    c0 = s["c0"]
                    act(Pb[i % 3][:, c0:512], ps[:, zb, c0:512], AF.Exp, reads=[psB[zb]], writes=[PB_[i % 3]], scale=sc)

                def mAV(i):
                    s = steps[i]
                    h, kt, c0 = s["h"], s["kt"], s["c0"]
                    ob = 6
                    mm(ps[:, ob, c0:512], Vm[:, kt, h * 128:(h + 1) * 128], Pb[i % 3][:, c0:512], s["first"], s["last"],
                       [VmB[kt], PB_[i % 3]], [psB[ob]])
                    db = 7
                    mm(ps[:, db, c0:512], ones, Pb[i % 3][:, c0:512], s["first"], s["last"], [PB_[i % 3], cstB], [psB[db]])
                    if s["last"]:
                        T.op("dve", lambda e: e.reciprocal(out=Pacc[1][:, :], in_=ps[:, db, :]), reads=[psB[db]],
                             writes=[PaccB[1]])
                        tt("dve", omT[:, h, :], ps[:, ob, :], Pacc[1][:, :], ALU.mult, [psB[ob], PaccB[1]], [omTB[h]])

                for it in range(-2, n):
                    if 0 <= it + 2 < n:
                        mZ(it + 2)
                    if 0 <= it + 1 < n:
                        mP(it + 1)
                    if 0 <= it < n:
                        mAV(it)
                    pull(gnext, -(-56 // n))

            def outnorm(j):
                blk = seq * NQB + j
                tt("pool", sq[:, :, :], omT[:, :, :], omT[:, :, :], ALU.mult, omTB, [sqB])
                b = nbk()
                for m in range(4):
                    mm(ps[:, b, :], ones, sq[:, m, :], m == 0, m == 3, [sqB, cstB], [psB[b]])
                rstd_big(ps[:, b, :], psB[b], rstd[0][:, :], rstdB[0], 1.0 / 512)
                for m in range(4):
                    stt("dve", mixo[:, m, :], omT[:, m, :], gpk[:, 15 + m:16 + m], rstd[0][:, :], ALU.mult, ALU.mult,
                        [omTB[m], rstdB[0], gpkB], [mixoB])
                dma(mixs[blk, :, 4:8, :], mixo[:, :, :], ("mix", 0), [mixoB], [])

            drain(prep(0))
            for j in range(NQB):
                gnext = prep(j + 1) if j + 1 < NQB else None
                run_steps(j, gnext)
                drain(gnext)
                outnorm(j)

        def phase_mlp():
            A = Arena(0)
            wo = A.take([128, 8, 1024], BF16)
            wup = A.take([128, 8, 4096], BF16)
            wdn = A.take([128, 32, 1024], BF16)
            gfin = A.take([128, 1024], F32)
            mixT = A.take([128, 8, 256], BF16)
            h1 = [A.take([128, 2, 1024], F32) for _ in range(2)]
            vtok = A.take([128, 1024], BF16)
            vT = [A.take([128, 8, 256], BF16) for _ in range(2)]
            hidT = A.take([128, 32, 256], BF16)
            hrall = A.take([128, 1024], BF16)
            hr = [hrall[:, q * 512:(q + 1) * 512].rearrange("p (a b) -> p a b", a=2) for q in range(2)]
            woB, wupB, wdnB, gfinB = T.buf("wo"), T.buf("wup"), T.buf("wdn"), T.buf("gfin")
            mixTB = T.buf("mixT")
            h1B = [T.bufs_n(f"h1_{p}_", 2) for p in range(2)]
            vtokB = T.buf("vtok")
            vTB = T.bufs_n("vT", 2)
            hidTB = T.bufs_n("hidT", 16)
            hrB = T.bufs_n("hr", 2)
            sB = [T.bufs_n(f"ms{q}_", 4) for q in range(3)]

            dma(gfin[:, :], gfin_d[0:1, :].partition_broadcast(128), ("c", 2), [], [gfinB])
            stg = [(h1[p][:, c, :], h1B[p][c]) for p in range(2) for c in range(2)]
            engs = ["dve", "pool", "act"]
            n = 0
            jobs = []
            for k in range(8):
                jobs.append((w_o[k * 128:(k + 1) * 128, :], wo[:, k, :], woB))
            for k in range(8):
                for q in range(4):
                    jobs.append((w_up[k * 128:(k + 1) * 128, q * 1024:(q + 1) * 1024], wup[:, k, q * 1024:(q + 1) * 1024], wupB))
            for f in range(32):
                jobs.append((w_dn[f * 128:(f + 1) * 128, :], wdn[:, f, :], wdnB))
            for src, dst, dB in jobs:
                sa, sBf = stg[n % 4]
                dma(sa, src, ("stg", n % 4), [], [sBf])
                eng = engs[n % 3]
                if eng == "act":
                    act(dst, sa, AF.Copy, reads=[sBf], writes=[dB])
                else:
                    tcopy(eng, dst, sa, [sBf], [dB])
                n += 1

            nb = NT // 256

            def stageA(bi):
                p = bi % 2
                t0 = bi * 256
                blk, half = bi // 2, bi % 2
                dma(mixT[:, :, :], mixs[blk, :, :, half * 256:(half + 1) * 256], ("mixl", 0), [], [mixTB])
                for c in range(2):
                    dma(h1[p][:, c, :], x[t0 + c * 128:t0 + (c + 1) * 128, :], ("xl", p * 2 + c), [], [h1B[p][c]])
                yield
                for c in range(2):
                    for hf in range(2):
                        b = next_bank(8)
                        for k in range(8):
                            mm(ps[:, b, :], mixT[:, k, c * 128:(c + 1) * 128], wo[:, k, hf * 512:(hf + 1) * 512],
                               k == 0, k == 7, [mixTB, woB], [psB[b]])
                        tt("dve", h1[p][:, c, hf * 512:(hf + 1) * 512], ps[:, b, :], h1[p][:, c, hf * 512:(hf + 1) * 512],
                           ALU.add, [psB[b]], [h1B[p][c]])
                        yield
                    act(vtok[:, :], h1[p][:, c, :], AF.Square, reads=[h1B[p][c]], writes=[vtokB, sB[0][c]],
                        accum=stat[:, c:c + 1])
                    rstd_small(stat[:, c:c + 1], stat[:, 4 + c:5 + c], stat[:, 8 + c:9 + c], 1.0 / 1024,
                               sB[0][c], sB[1][c], sB[2][c])
                    yield
                    ts("dve", vtok[:, :], h1[p][:, c, :], stat[:, 8 + c:9 + c], None, ALU.mult, None,
                       [h1B[p][c], sB[2][c]], [vtokB])
                    b = next_bank(8)
                    tpv = ps[:, b, :].bitcast(BF16)
                    for k in range(8):
                        T.op("pe", (lambda o, i: (lambda e: e.transpose(o, i, ident)))(tpv[:, k * 128:(k + 1) * 128],
                                                                                         vtok[:, k * 128:(k + 1) * 128]),
                             reads=[vtokB, cstB], writes=[psB[b]])
                    yield
                    for k in range(8):
                        ts("dve", vT[p][:, k, c * 128:(c + 1) * 128], tpv[:, k * 128:(k + 1) * 128],
                           gpk[:, 19 + k:20 + k], None, ALU.mult, None, [psB[b], gpkB], [vTB[p]])
                    yield

            def stageB(bi, gnext):
                p = bi % 2
                t0 = bi * 256
                for f2 in range(16):
                    b = next_bank(8)
                    for q in range(2):
                        f = 2 * f2 + q
                        for k in range(8):
                            mm(ps[:, b, q * 256:(q + 1) * 256], wup[:, k, f * 128:(f + 1) * 128], vT[p][:, k, :],
                               k == 0, k == 7, [wupB, vTB[p]], [psB[b]])
                    hq = f2 % 2
                    act(hr[hq][:, :, :], ps[:, b, :].rearrange("p (a b) -> p a b", a=2), AF.Relu, reads=[psB[b]],
                        writes=[hrB[hq]])
                    tt("pool", hidT[:, 2 * f2:2 * f2 + 2, :], hr[hq][:, :, :], hr[hq][:, :, :], ALU.mult,
                       [hrB[hq]], [hidTB[f2]])
                    if f2 % 2 == 1:
                        pull(gnext, 1)
                for c in range(2):
                    for hf in range(2):
                        b = next_bank(8)
                        for f in range(32):
                            mm(ps[:, b, :], hidT[:, f, c * 128:(c + 1) * 128], wdn[:, f, hf * 512:(hf + 1) * 512],
                               f == 0, f == 31, [hidTB[f // 2], wdnB], [psB[b]])
                        tt("dve", h1[p][:, c, hf * 512:(hf + 1) * 512], ps[:, b, :], h1[p][:, c, hf * 512:(hf + 1) * 512],
                           ALU.add, [psB[b]], [h1B[p][c]])
                        pull(gnext, 2)
                    act(hrall[:, :], h1[p][:, c, :], AF.Square, reads=[h1B[p][c]], writes=[hrB[0], hrB[1], sB[0][2 + c]],
                        accum=stat[:, 2 + c:3 + c])
                    rstd_small(stat[:, 2 + c:3 + c], stat[:, 6 + c:7 + c], stat[:, 10 + c:11 + c], 1.0 / 1024,
                               sB[0][2 + c], sB[1][2 + c], sB[2][2 + c])
                    stt("dve", h1[p][:, c, :], h1[p][:, c, :], stat[:, 10 + c:11 + c], gfin[:, :], ALU.mult, ALU.mult,
                        [sB[2][2 + c], gfinB], [h1B[p][c]])
                    dma(out[t0 + c * 128:t0 + (c + 1) * 128, :], h1[p][:, c, :], ("out", p * 2 + c), [h1B[p][c]], [])

            drain(stageA(0))
            for bi in range(nb):
                gnext = stageA(bi + 1) if bi + 1 < nb else None
                stageB(bi, gnext)
                drain(gnext)

        for seq in range(NSEQ):
            phase_sb(seq)
            T.barrier()
            phase_mla(seq)
            T.barrier()
        phase_mlp()
        T.emit(nc, es)
    return nc


def _consts():
    bf = ml_dtypes.bfloat16
    j = np.arange(128)[:, None]
    s = np.arange(128)[None, :]
    ident = (j == s).astype(np.float32)
    triU = (j >= s).astype(np.float32)
    ones = np.ones((128, 128), np.float32)
    maskSB = np.where(j >= s, NEG, 0.0)
    maskML = np.where(j > s, NEG, 0.0)
    Dmat = (j == s - 1).astype(np.float32) - (j == s).astype(np.float32)
    Dprev = ((j == 127) & (s == 0)).astype(np.float32)
    ShiftM = (j == s - 1).astype(np.float32)
    maskCp = -maskSB
    return np.concatenate([ident, triU, ones, maskSB, maskML, Dmat, Dprev, ShiftM, maskCp], 1).astype(bf)


def _gpk(attn_g, qa_g, kva_g, sb_g, mla_g, mlp_g):
    g = np.zeros((128, 32), np.float32)
    g[:, 0:8] = attn_g.reshape(8, 128).T
    g[:, 8:10] = qa_g.reshape(2, 128).T
    g[:, 10:11] = kva_g.reshape(1, 128).T
    g[:, 11:15] = sb_g.reshape(4, 128).T
    g[:, 15:19] = mla_g.reshape(4, 128).T
    g[:, 19:27] = mlp_g.reshape(8, 128).T
    half = 32
    inv = (10000.0 ** (-np.arange(half, dtype=np.float32) / half)).astype(np.float32)
    g[0:64, 27] = np.concatenate([inv, inv])
    return g


_NC_CACHE = {}


def run(x, positions, attn_norm_g, w_in, q_a_norm_g, w_q_b, kv_a_norm_g, w_kv_b, sb_out_norm_g, mla_out_norm_g,
        w_o, mlp_norm_g, w_up, w_down, final_norm_g, n_cores=N_CORES, dbg=False):
    x = np.asarray(x)
    B, S, Dm = x.shape
    nseq = B // n_cores
    nqb = S // 512
    key = (nseq, nqb, dbg)
    if key not in _NC_CACHE:
        _NC_CACHE[key] = build(nseq, nqb, dbg)
    nc = _NC_CACHE[key]
    f32 = lambda a: np.ascontiguousarray(np.asarray(a), dtype=np.float32)
    shared = {
        "w_in": f32(w_in)[0], "w_qb": f32(w_q_b)[0], "w_kvb": f32(w_kv_b)[0], "w_o": f32(w_o)[0],
        "w_up": f32(w_up)[0], "w_dn": f32(w_down)[0],
        "gpk": _gpk(f32(attn_norm_g)[0], f32(q_a_norm_g)[0], f32(kv_a_norm_g)[0], f32(sb_out_norm_g)[0],
                    f32(mla_out_norm_g)[0], f32(mlp_norm_g)[0]),
        "gfin": f32(final_norm_g).reshape(1, 1024),
        "cstb": _consts(),
    }
    positions = np.ascontiguousarray(np.asarray(positions), dtype=np.int32)
    in_maps = []
    for c in range(n_cores):
        m = dict(shared)
        m["x"] = np.ascontiguousarray(x[c * nseq:(c + 1) * nseq].reshape(nseq * S, Dm), dtype=np.float32)
        m["pos"] = np.ascontiguousarray(positions[c * nseq:(c + 1) * nseq])
        in_maps.append(m)
    res = run_bass_kernel_spmd(nc, in_maps, core_ids=list(range(n_cores)))
    outs = [np.asarray(r["out"]).reshape(nseq, S, Dm) for r in res.results]
    full = np.concatenate(outs, 0).astype(np.float32)
    if dbg:
        return full, [np.asarray(r["mixs"]) for r in res.results]
    return full


def kernel(**inputs):
    return run(**inputs)
```

```python
import math
from contextlib import ExitStack

import ml_dtypes
import numpy as np

import concourse.bass as bass
import concourse.mybir as mybir
from concourse.bass_utils import run_bass_kernel_spmd

F32 = mybir.dt.float32
BF16 = mybir.dt.bfloat16
I32 = mybir.dt.int32
AF = mybir.ActivationFunctionType
ALU = mybir.AluOpType

N_CORES = 8
ARENA_BYTES = 200704
EPS = 1e-6
TWO_PI = 2.0 * math.pi
C1 = 6.28125
C2 = TWO_PI - C1
NEG = -30000.0


class Buf:
    __slots__ = ("name", "w", "rs")

    def __init__(self, name):
        self.name = name
        self.w = None
        self.rs = []


class Trk:
    ENGS = ("pe", "act", "dve", "pool", "sp")

    def __init__(self):
        self.ins = {e: [] for e in self.ENGS}
        self.known = {e: {} for e in self.ENGS}
        self.pending = {e: [] for e in self.ENGS}
        self.epoch = 0
        self.dma_cnt = {}
        self.bufs = []

    def buf(self, name):
        b = Buf(name)
        self.bufs.append(b)
        return b

    def bufs_n(self, name, n):
        return [self.buf(f"{name}{i}") for i in range(n)]

    def _collect(self, eng, reads, writes):
        deps = set()
        for b in reads:
            if b.w is not None:
                deps.add(b.w)
        for b in writes:
            if b.w is not None:
                deps.add(b.w)
            deps.update(b.rs)
        idx = len(self.ins[eng])
        best = {}
        for t in deps:
            if t[0] == "e":
                if t[1] == eng:
                    if eng in ("pe", "sp") or idx - t[2] > 2:
                        continue
                key = ("e", t[1])
            else:
                key = ("d", t[1])
            if key not in best or best[key] < t[2]:
                best[key] = t[2]
        waits = list(self.pending[eng])
        self.pending[eng] = []
        kn = self.known[eng]
        for key, v in best.items():
            if key[0] == "e" and key[1] == eng:
                waits.append(("e", eng, v))
                self.ins[eng][v]["signal"] = True
                continue
            if kn.get(key, -1) >= v:
                continue
            kn[key] = v
            waits.append((key[0], key[1], v))
            if key[0] == "e":
                self.ins[key[1]][v]["signal"] = True
        return waits

    def op(self, eng, fn, reads=(), writes=()):
        waits = self._collect(eng, reads, writes)
        idx = len(self.ins[eng])
        tok = ("e", eng, idx)
        self.ins[eng].append(dict(fn=fn, waits=waits, signal=False, epoch=self.epoch, dma=None))
        ws = set(id(b) for b in writes)
        for b in writes:
            b.w = tok
            b.rs = []
        for b in reads:
            if id(b) not in ws:
                b.rs.append(tok)
        return tok

    def dma(self, fn, key, reads=(), writes=()):
        waits = self._collect("sp", reads, writes)
        cnt = self.dma_cnt.get(key, 0) + 1
        self.dma_cnt[key] = cnt
        tok = ("d", key, cnt)
        self.ins["sp"].append(dict(fn=fn, waits=waits, signal=False, epoch=self.epoch, dma=key))
        ws = set(id(b) for b in writes)
        for b in writes:
            b.w = tok
            b.rs = []
        for b in reads:
            if id(b) not in ws:
                b.rs.append(tok)
        return tok

    def barrier(self, new_epoch=True):
        for e in self.ENGS:
            kn = self.known[e]
            for s in self.ENGS:
                if s == e or s == "sp" or not self.ins[s]:
                    continue
                v = len(self.ins[s]) - 1
                if kn.get(("e", s), -1) < v:
                    kn[("e", s)] = v
                    self.pending[e].append(("e", s, v))
                    self.ins[s][v]["signal"] = True
            for key, cnt in self.dma_cnt.items():
                if kn.get(("d", key), 0) < cnt:
                    kn[("d", key)] = cnt
                    self.pending[e].append(("d", key, cnt))
        for b in self.bufs:
            b.w = None
            b.rs = []
        if new_epoch:
            self.epoch += 1

    def emit(self, nc, es):
        esem = {}
        for e in self.ENGS:
            if e == "sp":
                continue
            for ep in sorted(set(i["epoch"] for i in self.ins[e])):
                esem[(e, ep)] = es.enter_context(nc.semaphore(f"s_{e}_{ep}"))
        dsem = {k: es.enter_context(nc.semaphore("d_" + "_".join(str(x) for x in k))) for k in self.dma_cnt}
        rank = {}
        for e in self.ENGS:
            cnt = {}
            r = []
            for i in self.ins[e]:
                if i["signal"]:
                    cnt[i["epoch"]] = cnt.get(i["epoch"], 0) + 1
                r.append(cnt.get(i["epoch"], 0))
            rank[e] = r
        final_waits = [("d", k, c) for k, c in self.dma_cnt.items()]
        block = es.enter_context(nc.Block())

        def run(e_name, eng):
            for i in self.ins[e_name]:
                for t in i["waits"]:
                    if t[0] == "e":
                        eng.wait_ge(esem[(t[1], self.ins[t[1]][t[2]]["epoch"])], rank[t[1]][t[2]])
                    else:
                        eng.wait_ge(dsem[t[1]], 16 * t[2])
                r = i["fn"](eng)
                if i["signal"]:
                    r.then_inc(esem[(e_name, i["epoch"])], 1)
                if i["dma"] is not None:
                    r.then_inc(dsem[i["dma"]], 16)
            for t in self.pending[e_name]:
                if t[0] == "e":
                    eng.wait_ge(esem[(t[1], self.ins[t[1]][t[2]]["epoch"])], rank[t[1]][t[2]])
                else:
                    eng.wait_ge(dsem[t[1]], 16 * t[2])
            if e_name == "sp":
                for t in final_waits:
                    eng.wait_ge(dsem[t[1]], 16 * t[2])

        @block.tensor
        def _(eng):
            run("pe", eng)

        @block.scalar
        def _(eng):
            run("act", eng)

        @block.vector
        def _(eng):
            run("dve", eng)

        @block.gpsimd
        def _(eng):
            run("pool", eng)

        @block.sync
        def _(eng):
            run("sp", eng)


def build(NSEQ, NQB, dbg=False):
    S = NQB * 512
    NT = NSEQ * S
    NBLK = NSEQ * NQB
    nc = bass.Bass("TRN2", target_bir_lowering=False)
    x = nc.dram_tensor("x", [NT, 1024], F32, kind="ExternalInput").ap()
    pos = nc.dram_tensor("pos", [NSEQ, S], I32, kind="ExternalInput").ap()
    w_in = nc.dram_tensor("w_in", [1024, 1984], F32, kind="ExternalInput").ap()
    w_qb = nc.dram_tensor("w_qb", [256, 768], F32, kind="ExternalInput").ap()
    w_kvb = nc.dram_tensor("w_kvb", [128, 1024], F32, kind="ExternalInput").ap()
    w_o = nc.dram_tensor("w_o", [1024, 1024], F32, kind="ExternalInput").ap()
    w_up = nc.dram_tensor("w_up", [1024, 4096], F32, kind="ExternalInput").ap()
    w_dn = nc.dram_tensor("w_dn", [4096, 1024], F32, kind="ExternalInput").ap()
    gpk_d = nc.dram_tensor("gpk", [128, 32], F32, kind="ExternalInput").ap()
    gfin_d = nc.dram_tensor("gfin", [1, 1024], F32, kind="ExternalInput").ap()
    cst_d = nc.dram_tensor("cstb", [128, 1152], BF16, kind="ExternalInput").ap()
    out = nc.dram_tensor("out", [NT, 1024], F32, kind="ExternalOutput").ap()
    mixs = nc.dram_tensor("mixs", [NBLK, 128, 8, 512], BF16,
                          kind="ExternalOutput" if dbg else "Internal").ap()

    T = Trk()
    with ExitStack() as es:
        arena = es.enter_context(nc.sbuf_tensor("arena", [128, ARENA_BYTES // 2], BF16))
        cst = es.enter_context(nc.sbuf_tensor("cst", [128, 1152], BF16))
        gpk = es.enter_context(nc.sbuf_tensor("gpkt", [128, 32], F32))
        epst = es.enter_context(nc.sbuf_tensor("epst", [128, 1], F32))
        stat = es.enter_context(nc.sbuf_tensor("stat", [128, 16], F32))
        ps = es.enter_context(nc.psum_tensor("ps", [128, 8, 512], F32))
        psB = T.bufs_n("ps", 8)
        cstB = T.buf("cst")
        statB = {}

        ident = cst[:, 0:128]
        triU = cst[:, 128:256]
        ones = cst[:, 256:384]
        maskSB = cst[:, 384:512]
        maskML = cst[:, 512:640]
        Dmat = cst[:, 640:768]
        Dprev = cst[:, 768:896]
        ShiftM = cst[:, 896:1024]
        maskCp = cst[:, 1024:1152]

        class Arena:
            def __init__(self, off=0):
                self.off = off

            def take(self, shape, dt):
                n = int(np.prod(shape[1:]))
                sz = 2 if dt == BF16 else 4
                nb = n * sz
                nb = (nb + 63) // 64 * 64
                assert self.off + nb <= ARENA_BYTES, (self.off, nb, shape)
                v = arena[0:shape[0], self.off // 2:self.off // 2 + n * sz // 2]
                self.off += nb
                if dt != BF16:
                    v = v.bitcast(dt)
                if len(shape) == 3:
                    v = v.rearrange("p (a b) -> p a b", a=shape[1])
                return v

        def mm(out_ap, lhsT, rhs, start, stop, reads, writes):
            T.op("pe", lambda e: e.matmul(out_ap, lhsT=lhsT, rhs=rhs, start=start, stop=stop),
                 reads=reads, writes=writes)

        def act(out_ap, in_ap, func, reads, writes, scale=1.0, bias=None, accum=None):
            kw = {}
            if bias is not None:
                kw["bias"] = bias
            if accum is not None:
                kw["accum_out"] = accum
            T.op("act", lambda e: e.activation(out=out_ap, in_=in_ap, func=func, scale=scale, **kw),
                 reads=reads, writes=writes)

        def tcopy(eng, out_ap, in_ap, reads, writes):
            T.op(eng, lambda e: e.tensor_copy(out=out_ap, in_=in_ap), reads=reads, writes=writes)

        def tt(eng, out_ap, in0, in1, op, reads, writes):
            T.op(eng, lambda e: e.tensor_tensor(out=out_ap, in0=in0, in1=in1, op=op), reads=reads, writes=writes)

        def ts(eng, out_ap, in0, s1, s2, op0, op1, reads, writes):
            if s2 is None:
                T.op(eng, lambda e: e.tensor_scalar(out=out_ap, in0=in0, scalar1=s1, scalar2=None, op0=op0),
                     reads=reads, writes=writes)
            else:
                T.op(eng, lambda e: e.tensor_scalar(out=out_ap, in0=in0, scalar1=s1, scalar2=s2, op0=op0, op1=op1),
                     reads=reads, writes=writes)

        def stt(eng, out_ap, in0, scalar, in1, op0, op1, reads, writes):
            T.op(eng, lambda e: e.scalar_tensor_tensor(out=out_ap, in0=in0, scalar=scalar, in1=in1, op0=op0, op1=op1),
                 reads=reads, writes=writes)

        def dma(out_ap, in_ap, key, reads, writes):
            T.dma(lambda e: e.dma_start(out=out_ap, in_=in_ap), key, reads=reads, writes=writes)

        def rstd_small(ssq_col, ln_col, rs_col, inv_n, bS, bL, bR):
            act(ln_col, ssq_col, AF.Ln, reads=[bS, epsB], writes=[bL], scale=inv_n, bias=epst[:, 0:1])
            act(rs_col, ln_col, AF.Exp, reads=[bL], writes=[bR], scale=-0.5)

        def rstd_big(ps_ap, psb, rs, rsB, inv_n, P=128):
            act(rs, ps_ap, AF.Ln, reads=[psb, epsB], writes=[rsB], scale=inv_n, bias=epst[0:P, 0:1])
            act(rs, rs, AF.Exp, reads=[], writes=[rsB], scale=-0.5)

        gpkB = T.buf("gpk")
        epsB = T.buf("eps")
        dma(cst[:, :], cst_d[:, :], ("c", 0), [], [cstB])
        dma(gpk[:, :], gpk_d[:, :], ("c", 1), [], [gpkB])
        T.op("pool", lambda e: e.memset(epst[:, :], EPS), writes=[epsB])

        A0 = Arena(0)
        wall = A0.take([128, 8, 1984], BF16)
        wsw = A0.take([128, 8, 64], BF16)
        wqb = A0.take([128, 2, 768], BF16)
        wqbs = A0.take([128, 2, 256], BF16)
        wkvb = A0.take([128, 1024], BF16)
        wkvv = A0.take([128, 512], BF16)
        W_END = A0.off
        wallB, wswB, wqbB, wqbsB, wkvbB, wkvvB = (T.buf(n) for n in ("wall", "wsw", "wqb", "wqbs", "wkvb", "wkvv"))

        def front_alloc(A):
            d = {}
            d["xt"] = [A.take([128, 1024], F32) for _ in range(2)]
            d["utok"] = A.take([128, 1024], BF16)
            d["uT"] = A.take([128, 8, 512], BF16)
            return d

        xtB = T.bufs_n("xt", 2)
        utokB = T.buf("utok")
        junkB = T.buf("junk")
        uTB = T.buf("uT")
        ssqB = T.bufs_n("ssq", 4)
        lnB = T.bufs_n("ln", 4)
        rsB = T.bufs_n("rs", 4)

        def load_weights_attn(fr):
            xt = fr["xt"]
            engs = ["dve", "pool"]
            n = 0
            for k in range(8):
                for hh, (c0, c1) in enumerate(((0, 1024), (1024, 1984))):
                    w = c1 - c0
                    dma(xt[hh][:, 0:w], w_in[k * 128:(k + 1) * 128, c0:c1], ("x", hh), [], [xtB[hh]])
                    tcopy(engs[n % 2], wall[:, k, c0:c1], xt[hh][:, 0:w], [xtB[hh]], [wallB])
                    n += 1
            for k in range(2):
                dma(xt[k][:, 0:768], w_qb[k * 128:(k + 1) * 128, :], ("x", k), [], [xtB[k]])
                tcopy(engs[k], wqb[:, k, :], xt[k][:, 0:768], [xtB[k]], [wqbB])
            dma(xt[0][:, :], w_kvb[:, :], ("x", 0), [], [xtB[0]])
            tcopy("dve", wkvb[:, :], xt[0][:, :], [xtB[0]], [wkvbB])
            ts("dve", wsw[:, :, 0:32], wall[:, :, 1952:1984], -1.0, None, ALU.mult, None, [wallB], [wswB])
            tcopy("dve", wsw[:, :, 32:64], wall[:, :, 1920:1952], [wallB], [wswB])
            for h in range(4):
                b = h * 192 + 128
                ts("dve", wqbs[:, :, h * 64:h * 64 + 32], wqb[:, :, b + 32:b + 64], -1.0, None, ALU.mult, None,
                   [wqbB], [wqbsB])
                tcopy("dve", wqbs[:, :, h * 64 + 32:h * 64 + 64], wqb[:, :, b:b + 32], [wqbB], [wqbsB])
                tcopy("pool", wkvv[:, h * 128:(h + 1) * 128], wkvb[:, h * 256 + 128:h * 256 + 256], [wkvbB], [wkvvB])

        def frontend(fr, t0, gcol0, tbank):
            xt, utok, uT = fr["xt"], fr["utok"], fr["uT"]
            for c in range(4):
                xb, xB = xt[c % 2], xtB[c % 2]
                dma(xb[:, :], x[t0 + c * 128:t0 + (c + 1) * 128, :], ("x", c % 2), [], [xB])
                act(utok[:, :], xb[:, :], AF.Square, reads=[xB], writes=[utokB, ssqB[c]], accum=stat[:, c:c + 1])
                rstd_small(stat[:, c:c + 1], stat[:, 4 + c:5 + c], stat[:, 8 + c:9 + c], 1.0 / 1024,
                           ssqB[c], lnB[c], rsB[c])
                yield
                ts("dve", utok[:, :], xb[:, :], stat[:, 8 + c:9 + c], None, ALU.mult, None, [xB, rsB[c]], [utokB])
                bank = tbank()
                tpv = ps[:, bank, :].bitcast(BF16)
                for k in range(8):
                    T.op("pe", (lambda o, i: (lambda e: e.transpose(o, i, ident)))(tpv[:, k * 128:(k + 1) * 128],
                                                                                     utok[:, k * 128:(k + 1) * 128]),
                         reads=[utokB, cstB], writes=[psB[bank]])
                    if k == 3:
                        yield
                yield
                for k in range(8):
                    ts("dve", uT[:, k, c * 128:(c + 1) * 128], tpv[:, k * 128:(k + 1) * 128],
                       gpk[:, gcol0 + k:gcol0 + k + 1], None, ALU.mult, None, [psB[bank], gpkB], [uTB])
                yield

        def pull(g, n):
            if g is None:
                return
            for _ in range(n):
                try:
                    next(g)
                except StopIteration:
                    return

        def drain(g):
            if g is None:
                return
            for _ in g:
                pass

        def chain8(out_ap, bank, lhs_fn, rhs_fn, extra_reads):
            for k in range(8):
                mm(out_ap, lhs_fn(k), rhs_fn(k), k == 0, k == 7, reads=extra_reads, writes=[psB[bank]])

        def chain8g(out_ap, bank, lhs_fn, rhs_fn, extra_reads):
            for k in range(8):
                mm(out_ap, lhs_fn(k), rhs_fn(k), k == 0, k == 7, reads=extra_reads, writes=[psB[bank]])
                if k == 3:
                    yield

        rr = [0]

        def next_bank(nb=4):
            b = rr[0] % nb
            rr[0] += 1
            return b

        def phase_sb(seq):
            A = Arena(W_END)
            kT = A.take([128, 4, S], BF16)
            DVc = A.take([128, NQB * 4, 512], BF16)
            fr = front_alloc(A)
            uT = fr["uT"]
            qT = [A.take([128, 4, 512], BF16) for _ in range(2)]
            Vcur = [A.take([128, 4, 512], BF16) for _ in range(2)]
            Vlast = [A.take([128, 512], BF16) for _ in range(3)]
            Eb = [A.take([128, 2, 512], F32) for _ in range(2)]
            spb = [A.take([128, 2, 512], BF16) for _ in range(3)]
            Wb = [A.take([128, 2, 512], BF16) for _ in range(2)]
            Sb = [A.take([128, 2, 512], BF16) for _ in range(2)]
            oT = A.take([128, 4, 512], F32)
            sq = A.take([128, 4, 512], BF16)
            rstd = A.take([128, 512], F32)
            mixo = [A.take([128, 4, 512], BF16) for _ in range(2)]
            kTB = [T.bufs_n(f"kT{m}_", NQB) for m in range(4)]
            DVB = T.bufs_n("DV", NQB * 4)
            qTB = [T.bufs_n(f"qT{p}_", 4) for p in range(2)]
            VcB = [T.bufs_n(f"Vc{p}_", 4) for p in range(2)]
            VlB = T.bufs_n("Vl", 3)
            EB = T.bufs_n("E", 2)
            SPB = T.bufs_n("sp", 3)
            WB = T.bufs_n("W", 2)
            SB_ = T.bufs_n("S", 2)
            oTB = T.bufs_n("oT", 4)
            sqB, rstdB = T.buf("sq"), T.buf("rstd")
            mixoB = T.bufs_n("mixo", 2)
            if seq == 0:
                load_weights_attn(fr)
            chain_ctr = [0]
            PB = 7

            def prep(j):
                par = j % 2
                t0 = seq * S + j * 512
                yield from frontend(fr, t0, 0, lambda: PB)
                b = PB
                for m in range(4):
                    yield from chain8g(ps[:, b, :], b, lambda k: wall[:, k, m * 128:(m + 1) * 128], lambda k: uT[:, k, :],
                           [wallB, uTB])
                    tcopy("dve", qT[par][:, m, :], ps[:, b, :], [psB[b]], [qTB[par][m]])
                    yield
                for m in range(4):
                    yield from chain8g(ps[:, b, :], b, lambda k: wall[:, k, 512 + m * 128:512 + (m + 1) * 128],
                           lambda k: uT[:, k, :], [wallB, uTB])
                    tcopy("dve", kT[:, m, j * 512:(j + 1) * 512], ps[:, b, :], [psB[b]], [kTB[m][j]])
                    yield
                for c in range(4):
                    yield from chain8g(ps[:, b, :], b, lambda k: uT[:, k, c * 128:(c + 1) * 128], lambda k: wall[:, k, 1024:1536],
                           [wallB, uTB])
                    tcopy("dve", Vcur[par][:, c, :], ps[:, b, :], [psB[b]], [VcB[par][c]])
                    yield
                tcopy("dve", Vlast[j % 3][:, :], Vcur[par][:, 3, :], [VcB[par][3]], [VlB[j % 3]])
                for c in range(4):
                    if c > 0:
                        prev, prevB = Vcur[par][:, c - 1, :], VcB[par][c - 1]
                    elif j > 0:
                        prev, prevB = Vlast[(j - 1) % 3][:, :], VlB[(j - 1) % 3]
                    else:
                        prev, prevB = None, None
                    mm(ps[:, b, :], Dmat, Vcur[par][:, c, :], True, prev is None, [VcB[par][c], cstB], [psB[b]])
                    if prev is not None:
                        mm(ps[:, b, :], Dprev, prev, False, True, [prevB, cstB], [psB[b]])
                    tcopy("dve", DVc[:, j * 4 + c, :], ps[:, b, :], [psB[b]], [DVB[j * 4 + c]])
                    yield

            def run_steps(j, gnext):
                par = j % 2
                steps = []
                for m in range(4):
                    ch = chain_ctr[0]
                    chain_ctr[0] += 1
                    for kt in range(4 * j + 3, -1, -1):
                        dg = kt - 4 * j
                        steps.append(dict(m=m, kt=kt, dg=dg, c0=128 * dg if dg >= 0 else 0,
                                          first=(kt == 4 * j + 3), last=(kt == 0), ch=ch))
                n = len(steps)

                def stZ(i):
                    s = steps[i]
                    zb = 2 * (i % 2)
                    m, kt, c0 = s["m"], s["kt"], s["c0"]
                    for e in range(2):
                        pr = slice(e * 64, (e + 1) * 64)
                        mm(ps[:, zb + e, c0:512], kT[pr, m, kt * 128:(kt + 1) * 128], qT[par][pr, m, c0:512],
                           True, s["dg"] < 0, [kTB[m][kt // 4], qTB[par][m]], [psB[zb + e]])
                        if s["dg"] >= 0:
                            mm(ps[:, zb + e, c0:c0 + 128], ident, maskSB, False, True, [cstB], [psB[zb + e]])

                def stE(i):
                    s = steps[i]
                    zb = 2 * (i % 2)
                    c0 = s["c0"]
                    act(Eb[i % 2][:, :, c0:512], ps[:, zb:zb + 2, c0:512], AF.Exp, reads=[psB[zb], psB[zb + 1]],
                        writes=[EB[i % 2]], scale=0.125)

                def stSP(i):
                    s = steps[i]
                    c0 = s["c0"]
                    act(spb[i % 3][:, :, c0:512], Eb[i % 2][:, :, c0:512], AF.Ln, reads=[EB[i % 2]],
                        writes=[SPB[i % 3]], bias=1.0)

                def stC(i):
                    s = steps[i]
                    c0 = s["c0"]
                    sp = s["ch"] % 2
                    for e in range(2):
                        mm(ps[:, 4 + e, c0:512], triU, spb[i % 3][:, e, c0:512], True, s["first"] and s["dg"] < 0,
                           [SPB[i % 3], cstB], [psB[4 + e]])
                        if not s["first"]:
                            mm(ps[:, 4 + e, c0:512], ones, Sb[sp][:, e, c0:512], False, s["dg"] < 0,
                               [SB_[sp], cstB], [psB[4 + e]])
                        if s["dg"] >= 0:
                            mm(ps[:, 4 + e, c0:c0 + 128], ident, maskCp, False, True, [cstB], [psB[4 + e]])

                def stS(i):
                    s = steps[i]
                    if s["last"]:
                        return
                    c0 = s["c0"]
                    sp = s["ch"] % 2
                    if s["first"]:
                        T.op("dve", lambda e: e.memset(Sb[sp][:, :, :], 0.0), writes=[SB_[sp]])
                    tt("dve", Sb[sp][:, :, c0:512], Sb[sp][:, :, c0:512], spb[i % 3][:, :, c0:512], ALU.add,
                       [SPB[i % 3]], [SB_[sp]])

                def stW(i):
                    s = steps[i]
                    c0 = s["c0"]
                    act(Wb[i % 2][:, :, c0:512], ps[:, 4:6, c0:512], AF.Exp, reads=[psB[4], psB[5]],
                        writes=[WB[i % 2]], scale=-1.0)

                def stAV(i):
                    s = steps[i]
                    m, kt, c0 = s["m"], s["kt"], s["c0"]
                    ob = 6
                    for e in range(2):
                        h = 2 * m + e
                        pr = slice(e * 64, (e + 1) * 64)
                        mm(ps[pr, ob, c0:512], DVc[:, kt, h * 64:(h + 1) * 64], Wb[i % 2][:, e, c0:512],
                           s["first"], s["last"] and s["dg"] < 0, [DVB[kt], WB[i % 2]], [psB[ob]])
                        if s["dg"] >= 0:
                            dg = s["dg"]
                            if dg > 0:
                                pv, pvB = Vcur[par][:, dg - 1, :], VcB[par][dg - 1]
                            elif j > 0:
                                pv, pvB = Vlast[(j - 1) % 3][:, :], VlB[(j - 1) % 3]
                            else:
                                pv, pvB = None, None
                            mm(ps[pr, ob, c0:c0 + 128], Vcur[par][:, dg, h * 64:(h + 1) * 64], ShiftM,
                               False, s["last"] and pv is None, [VcB[par][dg], cstB], [psB[ob]])
                            if pv is not None:
                                mm(ps[pr, ob, c0:c0 + 128], pv[:, h * 64:(h + 1) * 64], Dprev,
                                   False, s["last"], [pvB, cstB], [psB[ob]])
                    if s["last"]:
                        tcopy("dve", oT[:, m, :], ps[:, ob, :], [psB[ob]], [oTB[m]])

                for it in range(-3, n):
                    if 0 <= it + 3 < n:
                        stZ(it + 3)
                    if 0 <= it + 2 < n:
                        stE(it + 2)
                    if 0 <= it + 1 < n:
                        stSP(it + 1)
                    if 0 <= it < n:
                        stW(it)
                        stAV(it)
                    if 0 <= it + 1 < n:
                        stC(it + 1)
                        stS(it + 1)
                    pull(gnext, -(-46 // n))

            def outnorm(j):
                par = j % 2
                blk = seq * NQB + j
                tt("pool", sq[:, :, :], oT[:, :, :], oT[:, :, :], ALU.mult, oTB, [sqB])
                b = PB
                for m in range(4):
                    mm(ps[:, b, :], ones, sq[:, m, :], m == 0, m == 3, [sqB, cstB], [psB[b]])
                rstd_big(ps[:, b, :], psB[b], rstd[:, :], rstdB, 1.0 / 512)
                for m in range(4):
                    stt("dve", mixo[par][:, m, :], oT[:, m, :], gpk[:, 11 + m:12 + m], rstd[:, :], ALU.mult, ALU.mult,
                        [oTB[m], rstdB, gpkB], [mixoB[par]])
                dma(mixs[blk, :, 0:4, :], mixo[par][:, :, :], ("mix", par), [mixoB[par]], [])

            drain(prep(0))
            for j in range(NQB):
                gnext = prep(j + 1) if j + 1 < NQB else None
                run_steps(j, gnext)
                outnorm(j)
                drain(gnext)

        def phase_mla(seq):
            A = Arena(W_END)
            knT = A.take([128, 4, S], BF16)
            krT = A.take([64, S], BF16)
            Vm = A.take([128, NQB * 4, 512], BF16)
            fr = front_alloc(A)
            uT = fr["uT"]
            cT = A.take([128, 3, 512], F32)
            sq = A.take([128, 4, 512], BF16)
            rstd = [A.take([128, 512], F32) for _ in range(2)]
            cn = A.take([128, 3, 512], BF16)
            ang = A.take([64, 512], F32)
            posi = ang.bitcast(I32)
            a2 = A.take([64, 512], F32)
            kf = A.take([64, 512], F32)
            ki = kf.bitcast(I32)
            tab = [A.take([64, 512], F32) for _ in range(2)]
            ro = A.take([64, 512], F32)
            rs_ = A.take([64, 512], F32)
            qnT = [A.take([128, 4, 512], BF16) for _ in range(2)]
            qrT = [A.take([64, 4, 512], BF16) for _ in range(2)]
            Pb = [A.take([128, 512], BF16) for _ in range(3)]
            Pacc = [A.take([128, 512], F32) for _ in range(2)]
            Pbf = A.take([128, 512], BF16)
            omT = A.take([128, 4, 512], F32)
            mixo = A.take([128, 4, 512], BF16)
            knTB = [T.bufs_n(f"knT{h}_", NQB) for h in range(4)]
            krTB = T.bufs_n("krT", NQB)
            VmB = T.bufs_n("Vm", NQB * 4)
            cTB = T.bufs_n("cT", 3)
            sqB = T.buf("sq")
            rstdB = T.bufs_n("rstd", 2)
            cnB = T.bufs_n("cn", 3)
            angB, a2B, kfB = (T.buf(nm) for nm in ("ang", "a2", "kf"))
            tabB = T.bufs_n("tab", 2)
            roB, rsB_ = T.buf("ro"), T.buf("rs_")
            qnTB = [T.bufs_n(f"qnT{p}_", 4) for p in range(2)]
            qrTB = [T.bufs_n(f"qrT{p}_", 4) for p in range(2)]
            PB_ = T.bufs_n("P", 3)
            PaccB = T.bufs_n("Pacc", 2)
            PbfB = T.buf("Pbf")
            omTB = T.bufs_n("omT", 4)
            mixoB = T.buf("mixo")
            chain_ctr = [0]
            nbk = lambda: next_bank(3)
            invf = gpk[0:64, 27:28]
            sin_t, cos_t = tab[0], tab[1]

            def rope(out_ap, outB, o_ap, oB, s_ap, sB):
                tt("dve", ro[:, :], o_ap, cos_t[:, :], ALU.mult, [oB, tabB[1]], [roB])
                tt("dve", rs_[:, :], s_ap, sin_t[:, :], ALU.mult, [sB, tabB[0]], [rsB_])
                tt("dve", out_ap, ro[:, :], rs_[:, :], ALU.add, [roB, rsB_], [outB])

            def prep(j):
                par = j % 2
                t0 = seq * S + j * 512
                yield from frontend(fr, t0, 0, nbk)
                dma(posi[:, :], pos[seq:seq + 1, j * 512:(j + 1) * 512].partition_broadcast(64), ("pos", 0), [], [angB])
                tcopy("dve", ang[:, :], posi[:, :], [], [angB])
                ts("dve", ang[:, :], ang[:, :], invf, None, ALU.mult, None, [gpkB], [angB])
                for ti, shift in enumerate((0.0, math.pi / 2)):
                    ts("dve", a2[:, :], ang[:, :], shift, None, ALU.add, None, [angB], [a2B])
                    ts("dve", kf[:, :], a2[:, :], 1.0 / TWO_PI, None, ALU.mult, None, [a2B], [kfB])
                    tcopy("dve", ki[:, :], kf[:, :], [], [kfB])
                    tcopy("dve", kf[:, :], ki[:, :], [], [kfB])
                    stt("dve", a2[:, :], kf[:, :], -C1, a2[:, :], ALU.mult, ALU.add, [kfB], [a2B])
                    stt("dve", a2[:, :], kf[:, :], -C2, a2[:, :], ALU.mult, ALU.add, [kfB], [a2B])
                    ts("dve", a2[:, :], a2[:, :], -3.141592, 3.141592, ALU.max, ALU.min, [], [a2B])
                    act(tab[ti][:, :], a2[:, :], AF.Sin, reads=[a2B], writes=[tabB[ti]])
                    yield
                for idx, col0 in enumerate((1536, 1664, 1792)):
                    b = nbk()
                    yield from chain8g(ps[:, b, :], b, lambda k: wall[:, k, col0:col0 + 128], lambda k: uT[:, k, :], [wallB, uTB])
                    tcopy("dve", cT[:, idx, :], ps[:, b, :], [psB[b]], [cTB[idx]])
                    yield
                tt("pool", sq[:, 0:3, :], cT[:, :, :], cT[:, :, :], ALU.mult, cTB, [sqB])
                bq = nbk()
                mm(ps[:, bq, :], ones, sq[:, 0, :], True, False, [sqB, cstB], [psB[bq]])
                mm(ps[:, bq, :], ones, sq[:, 1, :], False, True, [sqB, cstB], [psB[bq]])
                bk = nbk()
                mm(ps[:, bk, :], ones, sq[:, 2, :], True, True, [sqB, cstB], [psB[bk]])
                rstd_big(ps[:, bq, :], psB[bq], rstd[0][:, :], rstdB[0], 1.0 / 256)
                rstd_big(ps[:, bk, :], psB[bk], rstd[1][:, :], rstdB[1], 1.0 / 128)
                yield
                for idx in range(3):
                    r = 0 if idx < 2 else 1
                    stt("dve", cn[:, idx, :], cT[:, idx, :], gpk[:, 8 + idx:9 + idx], rstd[r][:, :], ALU.mult, ALU.mult,
                        [cTB[idx], rstdB[r], gpkB], [cnB[idx]])
                yield
                b1 = nbk()
                yield from chain8g(ps[0:64, b1, :], b1, lambda k: wall[:, k, 1920:1984], lambda k: uT[:, k, :], [wallB, uTB])
                b2 = nbk()
                yield from chain8g(ps[0:64, b2, :], b2, lambda k: wsw[:, k, :], lambda k: uT[:, k, :], [wswB, uTB])
                rope(krT[:, j * 512:(j + 1) * 512], krTB[j], ps[0:64, b1, :], psB[b1], ps[0:64, b2, :], psB[b2])
                yield
                for h in range(4):
                    b = nbk()
                    for k in range(2):
                        mm(ps[:, b, :], wqb[:, k, h * 192:h * 192 + 128], cn[:, k, :], k == 0, k == 1,
                           [wqbB, cnB[k]], [psB[b]])
                    tcopy("dve", qnT[par][:, h, :], ps[:, b, :], [psB[b]], [qnTB[par][h]])
                    b1 = nbk()
                    for k in range(2):
                        mm(ps[0:64, b1, :], wqb[:, k, h * 192 + 128:h * 192 + 192], cn[:, k, :], k == 0, k == 1,
                           [wqbB, cnB[k]], [psB[b1]])
                    b2 = nbk()
                    for k in range(2):
                        mm(ps[0:64, b2, :], wqbs[:, k, h * 64:(h + 1) * 64], cn[:, k, :], k == 0, k == 1,
                           [wqbsB, cnB[k]], [psB[b2]])
                    rope(qrT[par][:, h, :], qrTB[par][h], ps[0:64, b1, :], psB[b1], ps[0:64, b2, :], psB[b2])
                    yield
                for h in range(4):
                    b = nbk()
                    mm(ps[:, b, :], wkvb[:, h * 256:h * 256 + 128], cn[:, 2, :], True, True, [wkvbB, cnB[2]], [psB[b]])
                    tcopy("dve", knT[:, h, j * 512:(j + 1) * 512], ps[:, b, :], [psB[b]], [knTB[h][j]])
                    yield
                for c in range(4):
                    b = nbk()
                    mm(ps[:, b, :], cn[:, 2, c * 128:(c + 1) * 128], wkvv[:, :], True, True, [wkvvB, cnB[2]], [psB[b]])
                    tcopy("dve", Vm[:, j * 4 + c, :], ps[:, b, :], [psB[b]], [VmB[j * 4 + c]])
                    yield

            def run_steps(j, gnext):
                par = j % 2
                steps = []
                for h in range(4):
                    ch = chain_ctr[0]
                    chain_ctr[0] += 1
                    for kt in range(4 * j + 3, -1, -1):
                        dg = kt - 4 * j
                        steps.append(dict(h=h, kt=kt, dg=dg, c0=128 * dg if dg >= 0 else 0,
                                          first=(kt == 4 * j + 3), last=(kt == 0), ch=ch))
                n = len(steps)
                sc = 192.0 ** -0.5

                def mZ(i):
                    s = steps[i]
                    zb = 4 + (i % 2)
                    h, kt, c0 = s["h"], s["kt"], s["c0"]
                    mm(ps[:, zb, c0:512], knT[:, h, kt * 128:(kt + 1) * 128], qnT[par][:, h, c0:512], True, False,
                       [knTB[h][kt // 4], qnTB[par][h]], [psB[zb]])
                    mm(ps[:, zb, c0:512], krT[:, kt * 128:(kt + 1) * 128], qrT[par][:, h, c0:512], False, s["dg"] < 0,
                       [krTB[kt // 4], qrTB[par][h]], [psB[zb]])
                    if s["dg"] >= 0:
                        mm(ps[:, zb, c0:c0 + 128], ident, maskML, False, True, [cstB], [psB[zb]])

                def mP(i):
                    s = steps[i]
                    zb = 4 + (i % 2)
                    c0 = s["c0"]
                    act(Pb[i % 3][:, c0:512], ps[:, zb, c0:512], AF.Exp, reads=[psB[zb]], writes=[PB_[i % 3]], scale=sc)

                def mAV(i):
                    s = steps[i]
                    h, kt, c0 = s["h"], s["kt"], s["c0"]
                    ob = 6 + (s["ch"] % 2)
                    mm(ps[:, ob, c0:512], Vm[:, kt, h * 128:(h + 1) * 128], Pb[i % 3][:, c0:512], s["first"], s["last"],
                       [VmB[kt], PB_[i % 3]], [psB[ob]])
                    db = 3
                    mm(ps[:, db, c0:512], ones, Pb[i % 3][:, c0:512], s["first"], s["last"], [PB_[i % 3], cstB], [psB[db]])
                    if s["last"]:
                        T.op("dve", lambda e: e.reciprocal(out=Pacc[1][:, :], in_=ps[:, db, :]), reads=[psB[db]],
                             writes=[PaccB[1]])
                        tt("dve", omT[:, h, :], ps[:, ob, :], Pacc[1][:, :], ALU.mult, [psB[ob], PaccB[1]], [omTB[h]])

                for it in range(-2, n):
                    if 0 <= it + 2 < n:
                        mZ(it + 2)
                    if 0 <= it + 1 < n:
                        mP(it + 1)
                    if 0 <= it < n:
                        mAV(it)
                    pull(gnext, -(-56 // n))

            def outnorm(j):
                blk = seq * NQB + j
                tt("pool", sq[:, :, :], omT[:, :, :], omT[:, :, :], ALU.mult, omTB, [sqB])
                b = nbk()
                for m in range(4):
                    mm(ps[:, b, :], ones, sq[:, m, :], m == 0, m == 3, [sqB, cstB], [psB[b]])
                rstd_big(ps[:, b, :], psB[b], rstd[0][:, :], rstdB[0], 1.0 / 512)
                for m in range(4):
                    stt("dve", mixo[:, m, :], omT[:, m, :], gpk[:, 15 + m:16 + m], rstd[0][:, :], ALU.mult, ALU.mult,
                        [omTB[m], rstdB[0], gpkB], [mixoB])
                dma(mixs[blk, :, 4:8, :], mixo[:, :, :], ("mix", 0), [mixoB], [])

            drain(prep(0))
            for j in range(NQB):
                gnext = prep(j + 1) if j + 1 < NQB else None
                run_steps(j, gnext)
                drain(gnext)
                outnorm(j)

        def phase_mlp():
            A = Arena(0)
            wo = A.take([128, 8, 1024], BF16)
            wup = A.take([128, 8, 4096], BF16)
            wdn = A.take([128, 32, 1024], BF16)
            gfin = A.take([128, 1024], F32)
            mixT = A.take([128, 8, 256], BF16)
            h1 = [A.take([128, 2, 1024], F32) for _ in range(2)]
            vtok = A.take([128, 1024], BF16)
            vT = [A.take([128, 8, 256], BF16) for _ in range(2)]
            hidT = A.take([128, 32, 256], BF16)
            hrall = A.take([128, 1024], BF16)
            hr = [hrall[:, q * 512:(q + 1) * 512].rearrange("p (a b) -> p a b", a=2) for q in range(2)]
            woB, wupB, wdnB, gfinB = T.buf("wo"), T.buf("wup"), T.buf("wdn"), T.buf("gfin")
            mixTB = T.buf("mixT")
            h1B = [T.bufs_n(f"h1_{p}_", 2) for p in range(2)]
            vtokB = T.buf("vtok")
            vTB = T.bufs_n("vT", 2)
            hidTB = T.bufs_n("hidT", 16)
            hrB = T.bufs_n("hr", 2)
            sB = [T.bufs_n(f"ms{q}_", 4) for q in range(3)]

            dma(gfin[:, :], gfin_d[0:1, :].partition_broadcast(128), ("c", 2), [], [gfinB])
            stg = [(h1[p][:, c, :], h1B[p][c]) for p in range(2) for c in range(2)]
            engs = ["dve", "pool", "act"]
            n = 0
            jobs = []
            for k in range(8):
                jobs.append((w_o[k * 128:(k + 1) * 128, :], wo[:, k, :], woB))
            for k in range(8):
                for q in range(4):
                    jobs.append((w_up[k * 128:(k + 1) * 128, q * 1024:(q + 1) * 1024], wup[:, k, q * 1024:(q + 1) * 1024], wupB))
            for f in range(32):
                jobs.append((w_dn[f * 128:(f + 1) * 128, :], wdn[:, f, :], wdnB))
            for src, dst, dB in jobs:
                sa, sBf = stg[n % 4]
                dma(sa, src, ("stg", n % 4), [], [sBf])
                eng = engs[n % 3]
                if eng == "act":
                    act(dst, sa, AF.Copy, reads=[sBf], writes=[dB])
                else:
                    tcopy(eng, dst, sa, [sBf], [dB])
                n += 1

            nb = NT // 256

            def stageA(bi):
                p = bi % 2
                t0 = bi * 256
                blk, half = bi // 2, bi % 2
                dma(mixT[:, :, :], mixs[blk, :, :, half * 256:(half + 1) * 256], ("mixl", 0), [], [mixTB])
                for c in range(2):
                    dma(h1[p][:, c, :], x[t0 + c * 128:t0 + (c + 1) * 128, :], ("xl", p * 2 + c), [], [h1B[p][c]])
                yield
                for c in range(2):
                    for hf in range(2):
                        b = next_bank(8)
                        for k in range(8):
                            mm(ps[:, b, :], mixT[:, k, c * 128:(c + 1) * 128], wo[:, k, hf * 512:(hf + 1) * 512],
                               k == 0, k == 7, [mixTB, woB], [psB[b]])
                        tt("dve", h1[p][:, c, hf * 512:(hf + 1) * 512], ps[:, b, :], h1[p][:, c, hf * 512:(hf + 1) * 512],
                           ALU.add, [psB[b]], [h1B[p][c]])
                        yield
                    act(vtok[:, :], h1[p][:, c, :], AF.Square, reads=[h1B[p][c]], writes=[vtokB, sB[0][c]],
                        accum=stat[:, c:c + 1])
                    rstd_small(stat[:, c:c + 1], stat[:, 4 + c:5 + c], stat[:, 8 + c:9 + c], 1.0 / 1024,
                               sB[0][c], sB[1][c], sB[2][c])
                    yield
                    ts("dve", vtok[:, :], h1[p][:, c, :], stat[:, 8 + c:9 + c], None, ALU.mult, None,
                       [h1B[p][c], sB[2][c]], [vtokB])
                    b = next_bank(8)
                    tpv = ps[:, b, :].bitcast(BF16)
                    for k in range(8):
                        T.op("pe", (lambda o, i: (lambda e: e.transpose(o, i, ident)))(tpv[:, k * 128:(k + 1) * 128],
                                                                                         vtok[:, k * 128:(k + 1) * 128]),
                             reads=[vtokB, cstB], writes=[psB[b]])
                    yield
                    for k in range(8):
                        ts("dve", vT[p][:, k, c * 128:(c + 1) * 128], tpv[:, k * 128:(k + 1) * 128],
                           gpk[:, 19 + k:20 + k], None, ALU.mult, None, [psB[b], gpkB], [vTB[p]])
                    yield

            def stageB(bi, gnext):
                p = bi % 2
                t0 = bi * 256
                for f2 in range(16):
                    b = next_bank(8)
                    for q in range(2):
                        f = 2 * f2 + q
                        for k in range(8):
                            mm(ps[:, b, q * 256:(q + 1) * 256], wup[:, k, f * 128:(f + 1) * 128], vT[p][:, k, :],
                               k == 0, k == 7, [wupB, vTB[p]], [psB[b]])
                    hq = f2 % 2
                    act(hr[hq][:, :, :], ps[:, b, :].rearrange("p (a b) -> p a b", a=2), AF.Relu, reads=[psB[b]],
                        writes=[hrB[hq]])
                    tt("pool", hidT[:, 2 * f2:2 * f2 + 2, :], hr[hq][:, :, :], hr[hq][:, :, :], ALU.mult,
                       [hrB[hq]], [hidTB[f2]])
                    if f2 % 2 == 1:
                        pull(gnext, 1)
                for c in range(2):
                    for hf in range(2):
                        b = next_bank(8)
                        for f in range(32):
                            mm(ps[:, b, :], hidT[:, f, c * 128:(c + 1) * 128], wdn[:, f, hf * 512:(hf + 1) * 512],
                               f == 0, f == 31, [hidTB[f // 2], wdnB], [psB[b]])
                        tt("dve", h1[p][:, c, hf * 512:(hf + 1) * 512], ps[:, b, :], h1[p][:, c, hf * 512:(hf + 1) * 512],
                           ALU.add, [psB[b]], [h1B[p][c]])
                        pull(gnext, 2)
                    act(hrall[:, :], h1[p][:, c, :], AF.Square, reads=[h1B[p][c]], writes=[hrB[0], hrB[1], sB[0][2 + c]],
                        accum=stat[:, 2 + c:3 + c])
                    rstd_small(stat[:, 2 + c:3 + c], stat[:, 6 + c:7 + c], stat[:, 10 + c:11 + c], 1.0 / 1024,
                               sB[0][2 + c], sB[1][2 + c], sB[2][2 + c])
                    stt("dve", h1[p][:, c, :], h1[p][:, c, :], stat[:, 10 + c:11 + c], gfin[:, :], ALU.mult, ALU.mult,
                        [sB[2][2 + c], gfinB], [h1B[p][c]])
                    dma(out[t0 + c * 128:t0 + (c + 1) * 128, :], h1[p][:, c, :], ("out", p * 2 + c), [h1B[p][c]], [])

            drain(stageA(0))
            for bi in range(nb):
                gnext = stageA(bi + 1) if bi + 1 < nb else None
                stageB(bi, gnext)
                drain(gnext)

        for seq in range(NSEQ):
            phase_sb(seq)
            T.barrier()
            phase_mla(seq)
            T.barrier()
        phase_mlp()
        T.emit(nc, es)
    return nc


def _consts():
    bf = ml_dtypes.bfloat16
    j = np.arange(128)[:, None]
    s = np.arange(128)[None, :]
    ident = (j == s).astype(np.float32)
    triU = (j >= s).astype(np.float32)
    ones = np.ones((128, 128), np.float32)
    maskSB = np.where(j >= s, NEG, 0.0)
    maskML = np.where(j > s, NEG, 0.0)
    Dmat = (j == s - 1).astype(np.float32) - (j == s).astype(np.float32)
    Dprev = ((j == 127) & (s == 0)).astype(np.float32)
    ShiftM = (j == s - 1).astype(np.float32)
    maskCp = -maskSB
    return np.concatenate([ident, triU, ones, maskSB, maskML, Dmat, Dprev, ShiftM, maskCp], 1).astype(bf)


def _gpk(attn_g, qa_g, kva_g, sb_g, mla_g, mlp_g):
    g = np.zeros((128, 32), np.float32)
    g[:, 0:8] = attn_g.reshape(8, 128).T
    g[:, 8:10] = qa_g.reshape(2, 128).T
    g[:, 10:11] = kva_g.reshape(1, 128).T
    g[:, 11:15] = sb_g.reshape(4, 128).T
    g[:, 15:19] = mla_g.reshape(4, 128).T
    g[:, 19:27] = mlp_g.reshape(8, 128).T
    half = 32
    inv = (10000.0 ** (-np.arange(half, dtype=np.float32) / half)).astype(np.float32)
    g[0:64, 27] = np.concatenate([inv, inv])
    return g


_NC_CACHE = {}


def run(x, positions, attn_norm_g, w_in, q_a_norm_g, w_q_b, kv_a_norm_g, w_kv_b, sb_out_norm_g, mla_out_norm_g,
        w_o, mlp_norm_g, w_up, w_down, final_norm_g, n_cores=N_CORES, dbg=False):
    x = np.asarray(x)
    B, S, Dm = x.shape
    nseq = B // n_cores
    nqb = S // 512
    key = (nseq, nqb, dbg)
    if key not in _NC_CACHE:
        _NC_CACHE[key] = build(nseq, nqb, dbg)
    nc = _NC_CACHE[key]
    f32 = lambda a: np.ascontiguousarray(np.asarray(a), dtype=np.float32)
    shared = {
        "w_in": f32(w_in)[0], "w_qb": f32(w_q_b)[0], "w_kvb": f32(w_kv_b)[0], "w_o": f32(w_o)[0],
        "w_up": f32(w_up)[0], "w_dn": f32(w_down)[0],
        "gpk": _gpk(f32(attn_norm_g)[0], f32(q_a_norm_g)[0], f32(kv_a_norm_g)[0], f32(sb_out_norm_g)[0],
                    f32(mla_out_norm_g)[0], f32(mlp_norm_g)[0]),
        "gfin": f32(final_norm_g).reshape(1, 1024),
        "cstb": _consts(),
    }
    positions = np.ascontiguousarray(np.asarray(positions), dtype=np.int32)
    in_maps = []
    for c in range(n_cores):
        m = dict(shared)
        m["x"] = np.ascontiguousarray(x[c * nseq:(c + 1) * nseq].reshape(nseq * S, Dm), dtype=np.float32)
        m["pos"] = np.ascontiguousarray(positions[c * nseq:(c + 1) * nseq])
        in_maps.append(m)
    res = run_bass_kernel_spmd(nc, in_maps, core_ids=list(range(n_cores)))
    outs = [np.asarray(r["out"]).reshape(nseq, S, Dm) for r in res.results]
    full = np.concatenate(outs, 0).astype(np.float32)
    if dbg:
        return full, [np.asarray(r["mixs"]) for r in res.results]
    return full


def kernel(**inputs):
    return run(**inputs)
```

```python
import math
from contextlib import ExitStack

import ml_dtypes
import numpy as np

import concourse.bass as bass
import concourse.mybir as mybir
from concourse.bass_utils import run_bass_kernel_spmd

F32 = mybir.dt.float32
BF16 = mybir.dt.bfloat16
I32 = mybir.dt.int32
AF = mybir.ActivationFunctionType
ALU = mybir.AluOpType

N_CORES = 8
ARENA_BYTES = 200704
EPS = 1e-6
TWO_PI = 2.0 * math.pi
C1 = 6.28125
C2 = TWO_PI - C1
NEG = -30000.0


class Buf:
    __slots__ = ("name", "w", "rs")

    def __init__(self, name):
        self.name = name
        self.w = None
        self.rs = []


class Trk:
    ENGS = ("pe", "act", "dve", "pool", "sp")

    def __init__(self):
        self.ins = {e: [] for e in self.ENGS}
        self.known = {e: {} for e in self.ENGS}
        self.pending = {e: [] for e in self.ENGS}
        self.epoch = 0
        self.dma_cnt = {}
        self.bufs = []

    def buf(self, name):
        b = Buf(name)
        self.bufs.append(b)
        return b

    def bufs_n(self, name, n):
        return [self.buf(f"{name}{i}") for i in range(n)]

    def _collect(self, eng, reads, writes):
        deps = set()
        for b in reads:
            if b.w is not None:
                deps.add(b.w)
        for b in writes:
            if b.w is not None:
                deps.add(b.w)
            deps.update(b.rs)
        idx = len(self.ins[eng])
        best = {}
        for t in deps:
            if t[0] == "e":
                if t[1] == eng:
                    if eng in ("pe", "sp") or idx - t[2] > 2:
                        continue
                key = ("e", t[1])
            else:
                key = ("d", t[1])
            if key not in best or best[key] < t[2]:
                best[key] = t[2]
        waits = list(self.pending[eng])
        self.pending[eng] = []
        kn = self.known[eng]
        for key, v in best.items():
            if key[0] == "e" and key[1] == eng:
                waits.append(("e", eng, v))
                self.ins[eng][v]["signal"] = True
                continue
            if kn.get(key, -1) >= v:
                continue
            kn[key] = v
            waits.append((key[0], key[1], v))
            if key[0] == "e":
                self.ins[key[1]][v]["signal"] = True
        return waits

    def op(self, eng, fn, reads=(), writes=()):
        waits = self._collect(eng, reads, writes)
        idx = len(self.ins[eng])
        tok = ("e", eng, idx)
        self.ins[eng].append(dict(fn=fn, waits=waits, signal=False, epoch=self.epoch, dma=None))
        ws = set(id(b) for b in writes)
        for b in writes:
            b.w = tok
            b.rs = []
        for b in reads:
            if id(b) not in ws:
                b.rs.append(tok)
        return tok

    def dma(self, fn, key, reads=(), writes=()):
        waits = self._collect("sp", reads, writes)
        cnt = self.dma_cnt.get(key, 0) + 1
        self.dma_cnt[key] = cnt
        tok = ("d", key, cnt)
        self.ins["sp"].append(dict(fn=fn, waits=waits, signal=False, epoch=self.epoch, dma=key))
        ws = set(id(b) for b in writes)
        for b in writes:
            b.w = tok
            b.rs = []
        for b in reads:
            if id(b) not in ws:
                b.rs.append(tok)
        return tok

    def barrier(self, new_epoch=True):
        for e in self.ENGS:
            kn = self.known[e]
            for s in self.ENGS:
                if s == e or s == "sp" or not self.ins[s]:
                    continue
                v = len(self.ins[s]) - 1
                if kn.get(("e", s), -1) < v:
                    kn[("e", s)] = v
                    self.pending[e].append(("e", s, v))
                    self.ins[s][v]["signal"] = True
            for key, cnt in self.dma_cnt.items():
                if kn.get(("d", key), 0) < cnt:
                    kn[("d", key)] = cnt
                    self.pending[e].append(("d", key, cnt))
        for b in self.bufs:
            b.w = None
            b.rs = []
        if new_epoch:
            self.epoch += 1

    def emit(self, nc, es):
        esem = {}
        for e in self.ENGS:
            if e == "sp":
                continue
            for ep in sorted(set(i["epoch"] for i in self.ins[e])):
                esem[(e, ep)] = es.enter_context(nc.semaphore(f"s_{e}_{ep}"))
        dsem = {k: es.enter_context(nc.semaphore("d_" + "_".join(str(x) for x in k))) for k in self.dma_cnt}
        rank = {}
        for e in self.ENGS:
            cnt = {}
            r = []
            for i in self.ins[e]:
                if i["signal"]:
                    cnt[i["epoch"]] = cnt.get(i["epoch"], 0) + 1
                r.append(cnt.get(i["epoch"], 0))
            rank[e] = r
        final_waits = [("d", k, c) for k, c in self.dma_cnt.items()]
        block = es.enter_context(nc.Block())

        def run(e_name, eng):
            for i in self.ins[e_name]:
                for t in i["waits"]:
                    if t[0] == "e":
                        eng.wait_ge(esem[(t[1], self.ins[t[1]][t[2]]["epoch"])], rank[t[1]][t[2]])
                    else:
                        eng.wait_ge(dsem[t[1]], 16 * t[2])
                r = i["fn"](eng)
                if i["signal"]:
                    r.then_inc(esem[(e_name, i["epoch"])], 1)
                if i["dma"] is not None:
                    r.then_inc(dsem[i["dma"]], 16)
            for t in self.pending[e_name]:
                if t[0] == "e":
                    eng.wait_ge(esem[(t[1], self.ins[t[1]][t[2]]["epoch"])], rank[t[1]][t[2]])
                else:
                    eng.wait_ge(dsem[t[1]], 16 * t[2])
            if e_name == "sp":
                for t in final_waits:
                    eng.wait_ge(dsem[t[1]], 16 * t[2])

        @block.tensor
        def _(eng):
            run("pe", eng)

        @block.scalar
        def _(eng):
            run("act", eng)

        @block.vector
        def _(eng):
            run("dve", eng)

        @block.gpsimd
        def _(eng):
            run("pool", eng)

        @block.sync
        def _(eng):
            run("sp", eng)


def build(NSEQ, NQB, dbg=False):
    S = NQB * 512
    NT = NSEQ * S
    NBLK = NSEQ * NQB
    nc = bass.Bass("TRN2", target_bir_lowering=False)
    x = nc.dram_tensor("x", [NT, 1024], F32, kind="ExternalInput").ap()
    pos = nc.dram_tensor("pos", [NSEQ, S], I32, kind="ExternalInput").ap()
    w_in = nc.dram_tensor("w_in", [1024, 1984], F32, kind="ExternalInput").ap()
    w_qb = nc.dram_tensor("w_qb", [256, 768], F32, kind="ExternalInput").ap()
    w_kvb = nc.dram_tensor("w_kvb", [128, 1024], F32, kind="ExternalInput").ap()
    w_o = nc.dram_tensor("w_o", [1024, 1024], F32, kind="ExternalInput").ap()
    w_up = nc.dram_tensor("w_up", [1024, 4096], F32, kind="ExternalInput").ap()
    w_dn = nc.dram_tensor("w_dn", [4096, 1024], F32, kind="ExternalInput").ap()
    gpk_d = nc.dram_tensor("gpk", [128, 32], F32, kind="ExternalInput").ap()
    gfin_d = nc.dram_tensor("gfin", [1, 1024], F32, kind="ExternalInput").ap()
    cst_d = nc.dram_tensor("cstb", [128, 1152], BF16, kind="ExternalInput").ap()
    out = nc.dram_tensor("out", [NT, 1024], F32, kind="ExternalOutput").ap()
    mixs = nc.dram_tensor("mixs", [NBLK, 128, 8, 512], BF16,
                          kind="ExternalOutput" if dbg else "Internal").ap()

    T = Trk()
    with ExitStack() as es:
        arena = es.enter_context(nc.sbuf_tensor("arena", [128, ARENA_BYTES // 2], BF16))
        cst = es.enter_context(nc.sbuf_tensor("cst", [128, 1152], BF16))
        gpk = es.enter_context(nc.sbuf_tensor("gpkt", [128, 32], F32))
        epst = es.enter_context(nc.sbuf_tensor("epst", [128, 1], F32))
        stat = es.enter_context(nc.sbuf_tensor("stat", [128, 16], F32))
        ps = es.enter_context(nc.psum_tensor("ps", [128, 8, 512], F32))
        psB = T.bufs_n("ps", 8)
        cstB = T.buf("cst")
        statB = {}

        ident = cst[:, 0:128]
        triU = cst[:, 128:256]
        ones = cst[:, 256:384]
        maskSB = cst[:, 384:512]
        maskML = cst[:, 512:640]
        Dmat = cst[:, 640:768]
        Dprev = cst[:, 768:896]
        ShiftM = cst[:, 896:1024]
        maskCp = cst[:, 1024:1152]

        class Arena:
            def __init__(self, off=0):
                self.off = off

            def take(self, shape, dt):
                n = int(np.prod(shape[1:]))
                sz = 2 if dt == BF16 else 4
                nb = n * sz
                nb = (nb + 63) // 64 * 64
                assert self.off + nb <= ARENA_BYTES, (self.off, nb, shape)
                v = arena[0:shape[0], self.off // 2:self.off // 2 + n * sz // 2]
                self.off += nb
                if dt != BF16:
                    v = v.bitcast(dt)
                if len(shape) == 3:
                    v = v.rearrange("p (a b) -> p a b", a=shape[1])
                return v

        def mm(out_ap, lhsT, rhs, start, stop, reads, writes, skip=False):
            if skip:
                T.op("pe", lambda e: e.matmul(out_ap, lhsT=lhsT, rhs=rhs, start=start, stop=stop, skip_group_check=True),
                     reads=reads, writes=writes)
            else:
                T.op("pe", lambda e: e.matmul(out_ap, lhsT=lhsT, rhs=rhs, start=start, stop=stop),
                     reads=reads, writes=writes)

        def act(out_ap, in_ap, func, reads, writes, scale=1.0, bias=None, accum=None):
            kw = {}
            if bias is not None:
                kw["bias"] = bias
            if accum is not None:
                kw["accum_out"] = accum
            T.op("act", lambda e: e.activation(out=out_ap, in_=in_ap, func=func, scale=scale, **kw),
                 reads=reads, writes=writes)

        def tcopy(eng, out_ap, in_ap, reads, writes):
            T.op(eng, lambda e: e.tensor_copy(out=out_ap, in_=in_ap), reads=reads, writes=writes)

        def tt(eng, out_ap, in0, in1, op, reads, writes):
            T.op(eng, lambda e: e.tensor_tensor(out=out_ap, in0=in0, in1=in1, op=op), reads=reads, writes=writes)

        def ts(eng, out_ap, in0, s1, s2, op0, op1, reads, writes):
            if s2 is None:
                T.op(eng, lambda e: e.tensor_scalar(out=out_ap, in0=in0, scalar1=s1, scalar2=None, op0=op0),
                     reads=reads, writes=writes)
            else:
                T.op(eng, lambda e: e.tensor_scalar(out=out_ap, in0=in0, scalar1=s1, scalar2=s2, op0=op0, op1=op1),
                     reads=reads, writes=writes)

        def stt(eng, out_ap, in0, scalar, in1, op0, op1, reads, writes):
            T.op(eng, lambda e: e.scalar_tensor_tensor(out=out_ap, in0=in0, scalar=scalar, in1=in1, op0=op0, op1=op1),
                 reads=reads, writes=writes)

        def dma(out_ap, in_ap, key, reads, writes):
            T.dma(lambda e: e.dma_start(out=out_ap, in_=in_ap), key, reads=reads, writes=writes)

        def rstd_small(ssq_col, ln_col, rs_col, inv_n, bS, bL, bR):
            act(ln_col, ssq_col, AF.Ln, reads=[bS, epsB], writes=[bL], scale=inv_n, bias=epst[:, 0:1])
            act(rs_col, ln_col, AF.Exp, reads=[bL], writes=[bR], scale=-0.5)

        def rstd_big(ps_ap, psb, rs, rsB, inv_n, P=128):
            act(rs, ps_ap, AF.Ln, reads=[psb, epsB], writes=[rsB], scale=inv_n, bias=epst[0:P, 0:1])
            act(rs, rs, AF.Exp, reads=[], writes=[rsB], scale=-0.5)

        gpkB = T.buf("gpk")
        epsB = T.buf("eps")
        dma(cst[:, :], cst_d[:, :], ("c", 0), [], [cstB])
        dma(gpk[:, :], gpk_d[:, :], ("c", 1), [], [gpkB])
        T.op("pool", lambda e: e.memset(epst[:, :], EPS), writes=[epsB])

        A0 = Arena(0)
        wall = A0.take([128, 8, 1984], BF16)
        wsw = A0.take([128, 8, 64], BF16)
        wqb = A0.take([128, 2, 768], BF16)
        wqbs = A0.take([128, 2, 256], BF16)
        wkvb = A0.take([128, 1024], BF16)
        wkvv = A0.take([128, 512], BF16)
        W_END = A0.off
        wallB, wswB, wqbB, wqbsB, wkvbB, wkvvB = (T.buf(n) for n in ("wall", "wsw", "wqb", "wqbs", "wkvb", "wkvv"))

        def front_alloc(A):
            d = {}
            d["xt"] = [A.take([128, 1024], F32) for _ in range(2)]
            d["utok"] = A.take([128, 1024], BF16)
            d["uT"] = A.take([128, 8, 512], BF16)
            return d

        xtB = T.bufs_n("xt", 2)
        utokB = T.buf("utok")
        junkB = T.buf("junk")
        uTB = T.buf("uT")
        ssqB = T.bufs_n("ssq", 4)
        lnB = T.bufs_n("ln", 4)
        rsB = T.bufs_n("rs", 4)

        def load_weights_attn(fr):
            xt = fr["xt"]
            engs = ["dve", "pool"]
            n = 0
            for k in range(8):
                for hh, (c0, c1) in enumerate(((0, 1024), (1024, 1984))):
                    w = c1 - c0
                    dma(xt[hh][:, 0:w], w_in[k * 128:(k + 1) * 128, c0:c1], ("x", hh), [], [xtB[hh]])
                    tcopy(engs[n % 2], wall[:, k, c0:c1], xt[hh][:, 0:w], [xtB[hh]], [wallB])
                    n += 1
            for k in range(2):
                dma(xt[k][:, 0:768], w_qb[k * 128:(k + 1) * 128, :], ("x", k), [], [xtB[k]])
                tcopy(engs[k], wqb[:, k, :], xt[k][:, 0:768], [xtB[k]], [wqbB])
            dma(xt[0][:, :], w_kvb[:, :], ("x", 0), [], [xtB[0]])
            tcopy("dve", wkvb[:, :], xt[0][:, :], [xtB[0]], [wkvbB])
            ts("dve", wsw[:, :, 0:32], wall[:, :, 1952:1984], -1.0, None, ALU.mult, None, [wallB], [wswB])
            tcopy("dve", wsw[:, :, 32:64], wall[:, :, 1920:1952], [wallB], [wswB])
            for h in range(4):
                b = h * 192 + 128
                ts("dve", wqbs[:, :, h * 64:h * 64 + 32], wqb[:, :, b + 32:b + 64], -1.0, None, ALU.mult, None,
                   [wqbB], [wqbsB])
                tcopy("dve", wqbs[:, :, h * 64 + 32:h * 64 + 64], wqb[:, :, b:b + 32], [wqbB], [wqbsB])
                tcopy("pool", wkvv[:, h * 128:(h + 1) * 128], wkvb[:, h * 256 + 128:h * 256 + 256], [wkvbB], [wkvvB])

        def frontend(fr, t0, gcol0, tbank):
            xt, utok, uT = fr["xt"], fr["utok"], fr["uT"]
            for c in range(4):
                xb, xB = xt[c % 2], xtB[c % 2]
                dma(xb[:, :], x[t0 + c * 128:t0 + (c + 1) * 128, :], ("x", c % 2), [], [xB])
                act(utok[:, :], xb[:, :], AF.Square, reads=[xB], writes=[utokB, ssqB[c]], accum=stat[:, c:c + 1])
                rstd_small(stat[:, c:c + 1], stat[:, 4 + c:5 + c], stat[:, 8 + c:9 + c], 1.0 / 1024,
                           ssqB[c], lnB[c], rsB[c])
                yield
                ts("dve", utok[:, :], xb[:, :], stat[:, 8 + c:9 + c], None, ALU.mult, None, [xB, rsB[c]], [utokB])
                yield
                bank = tbank()
                tpv = ps[:, bank, :].bitcast(BF16)
                for k in range(8):
                    T.op("pe", (lambda o, i: (lambda e: e.transpose(o, i, ident)))(tpv[:, k * 128:(k + 1) * 128],
                                                                                     utok[:, k * 128:(k + 1) * 128]),
                         reads=[utokB, cstB], writes=[psB[bank]])
                    if k == 3:
                        yield
                yield
                for k in range(8):
                    ts("dve", uT[:, k, c * 128:(c + 1) * 128], tpv[:, k * 128:(k + 1) * 128],
                       gpk[:, gcol0 + k:gcol0 + k + 1], None, ALU.mult, None, [psB[bank], gpkB], [uTB])
                yield

        def pull(g, n):
            if g is None:
                return
            for _ in range(n):
                try:
                    next(g)
                except StopIteration:
                    return

        def drain(g):
            if g is None:
                return
            for _ in g:
                pass

        def chain8(out_ap, bank, lhs_fn, rhs_fn, extra_reads):
            for k in range(8):
                mm(out_ap, lhs_fn(k), rhs_fn(k), k == 0, k == 7, reads=extra_reads, writes=[psB[bank]])

        def chain8g(out_ap, bank, lhs_fn, rhs_fn, extra_reads):
            for k in range(8):
                mm(out_ap, lhs_fn(k), rhs_fn(k), k == 0, k == 7, reads=extra_reads, writes=[psB[bank]])
                if k == 3:
                    yield

        rr = [0]

        def next_bank(nb=4):
            b = rr[0] % nb
            rr[0] += 1
            return b

        def phase_sb(seq):
            A = Arena(W_END)
            kT = A.take([128, 4, S], BF16)
            DVc = A.take([128, NQB * 4, 512], BF16)
            fr = front_alloc(A)
            uT = fr["uT"]
            qT = [A.take([128, 4, 512], BF16) for _ in range(2)]
            Vcur = [A.take([128, 4, 512], BF16) for _ in range(2)]
            Vlast = [A.take([128, 512], BF16) for _ in range(3)]
            Eb = [A.take([128, 2, 512], F32) for _ in range(2)]
            spb = [A.take([128, 2, 512], BF16) for _ in range(3)]
            Wb = [A.take([128, 2, 512], BF16) for _ in range(2)]
            Sb = [A.take([128, 2, 512], BF16) for _ in range(2)]
            oT = A.take([128, 4, 512], F32)
            sq = A.take([128, 4, 512], BF16)
            rstd = A.take([128, 512], F32)
            mixo = [A.take([128, 4, 512], BF16) for _ in range(2)]
            kTB = [T.bufs_n(f"kT{m}_", NQB) for m in range(4)]
            DVB = T.bufs_n("DV", NQB * 4)
            qTB = [T.bufs_n(f"qT{p}_", 4) for p in range(2)]
            VcB = [T.bufs_n(f"Vc{p}_", 4) for p in range(2)]
            VlB = T.bufs_n("Vl", 3)
            EB = T.bufs_n("E", 2)
            SPB = T.bufs_n("sp", 3)
            WB = T.bufs_n("W", 2)
            SB_ = T.bufs_n("S", 2)
            oTB = T.bufs_n("oT", 4)
            sqB, rstdB = T.buf("sq"), T.buf("rstd")
            mixoB = T.bufs_n("mixo", 2)
            if seq == 0:
                load_weights_attn(fr)
            chain_ctr = [0]
            PB = 7

            def prep(j):
                par = j % 2
                t0 = seq * S + j * 512
                yield from frontend(fr, t0, 0, lambda: PB)
                b = PB
                for m in range(4):
                    yield from chain8g(ps[:, b, :], b, lambda k: wall[:, k, m * 128:(m + 1) * 128], lambda k: uT[:, k, :],
                           [wallB, uTB])
                    tcopy("dve", qT[par][:, m, :], ps[:, b, :], [psB[b]], [qTB[par][m]])
                    yield
                for m in range(4):
                    yield from chain8g(ps[:, b, :], b, lambda k: wall[:, k, 512 + m * 128:512 + (m + 1) * 128],
                           lambda k: uT[:, k, :], [wallB, uTB])
                    tcopy("dve", kT[:, m, j * 512:(j + 1) * 512], ps[:, b, :], [psB[b]], [kTB[m][j]])
                    yield
                for c in range(4):
                    yield from chain8g(ps[:, b, :], b, lambda k: uT[:, k, c * 128:(c + 1) * 128], lambda k: wall[:, k, 1024:1536],
                           [wallB, uTB])
                    tcopy("dve", Vcur[par][:, c, :], ps[:, b, :], [psB[b]], [VcB[par][c]])
                    yield
                tcopy("dve", Vlast[j % 3][:, :], Vcur[par][:, 3, :], [VcB[par][3]], [VlB[j % 3]])
                for c in range(4):
                    if c > 0:
                        prev, prevB = Vcur[par][:, c - 1, :], VcB[par][c - 1]
                    elif j > 0:
                        prev, prevB = Vlast[(j - 1) % 3][:, :], VlB[(j - 1) % 3]
                    else:
                        prev, prevB = None, None
                    mm(ps[:, b, :], Dmat, Vcur[par][:, c, :], True, prev is None, [VcB[par][c], cstB], [psB[b]])
                    if prev is not None:
                        mm(ps[:, b, :], Dprev, prev, False, True, [prevB, cstB], [psB[b]])
                    tcopy("dve", DVc[:, j * 4 + c, :], ps[:, b, :], [psB[b]], [DVB[j * 4 + c]])
                    yield

            def run_steps(j, gnext):
                par = j % 2
                steps = []
                for m in range(4):
                    ch = chain_ctr[0]
                    chain_ctr[0] += 1
                    for kt in range(4 * j + 3, -1, -1):
                        dg = kt - 4 * j
                        steps.append(dict(m=m, kt=kt, dg=dg, c0=128 * dg if dg >= 0 else 0,
                                          first=(kt == 4 * j + 3), last=(kt == 0), ch=ch))
                n = len(steps)

                def stZ(i):
                    s = steps[i]
                    zb = 2 * (i % 2)
                    m, kt, c0 = s["m"], s["kt"], s["c0"]
                    for e in range(2):
                        pr = slice(e * 64, (e + 1) * 64)
                        mm(ps[:, zb + e, c0:512], kT[pr, m, kt * 128:(kt + 1) * 128], qT[par][pr, m, c0:512],
                           True, s["dg"] < 0, [kTB[m][kt // 4], qTB[par][m]], [psB[zb + e]])
                        if s["dg"] >= 0:
                            mm(ps[:, zb + e, c0:c0 + 128], ident, maskSB, False, True, [cstB], [psB[zb + e]])

                def stE(i):
                    s = steps[i]
                    zb = 2 * (i % 2)
                    c0 = s["c0"]
                    act(Eb[i % 2][:, :, c0:512], ps[:, zb:zb + 2, c0:512], AF.Exp, reads=[psB[zb], psB[zb + 1]],
                        writes=[EB[i % 2]], scale=0.125)

                def stSP(i):
                    s = steps[i]
                    c0 = s["c0"]
                    act(spb[i % 3][:, :, c0:512], Eb[i % 2][:, :, c0:512], AF.Ln, reads=[EB[i % 2]],
                        writes=[SPB[i % 3]], bias=1.0)

                def stC(i):
                    s = steps[i]
                    c0 = s["c0"]
                    sp = s["ch"] % 2
                    for e in range(2):
                        mm(ps[:, 4 + e, c0:512], triU, spb[i % 3][:, e, c0:512], True, s["first"] and s["dg"] < 0,
                           [SPB[i % 3], cstB], [psB[4 + e]])
                        if not s["first"]:
                            mm(ps[:, 4 + e, c0:512], ones, Sb[sp][:, e, c0:512], False, s["dg"] < 0,
                               [SB_[sp], cstB], [psB[4 + e]])
                        if s["dg"] >= 0:
                            mm(ps[:, 4 + e, c0:c0 + 128], ident, maskCp, False, True, [cstB], [psB[4 + e]])

                def stS(i):
                    s = steps[i]
                    if s["last"]:
                        return
                    c0 = s["c0"]
                    sp = s["ch"] % 2
                    if s["first"]:
                        T.op("dve", lambda e: e.memset(Sb[sp][:, :, :], 0.0), writes=[SB_[sp]])
                    tt("dve", Sb[sp][:, :, c0:512], Sb[sp][:, :, c0:512], spb[i % 3][:, :, c0:512], ALU.add,
                       [SPB[i % 3]], [SB_[sp]])

                def stW(i):
                    s = steps[i]
                    c0 = s["c0"]
                    act(Wb[i % 2][:, :, c0:512], ps[:, 4:6, c0:512], AF.Exp, reads=[psB[4], psB[5]],
                        writes=[WB[i % 2]], scale=-1.0)

                def stAV(i):
                    s = steps[i]
                    m, kt, c0 = s["m"], s["kt"], s["c0"]
                    ob = 6
                    for e in range(2):
                        h = 2 * m + e
                        pr = slice(e * 64, (e + 1) * 64)
                        mm(ps[pr, ob, c0:512], DVc[:, kt, h * 64:(h + 1) * 64], Wb[i % 2][:, e, c0:512],
                           s["first"], s["last"] and s["dg"] < 0, [DVB[kt], WB[i % 2]], [psB[ob]], skip=True)
                        if s["dg"] >= 0:
                            dg = s["dg"]
                            if dg > 0:
                                pv, pvB = Vcur[par][:, dg - 1, :], VcB[par][dg - 1]
                            elif j > 0:
                                pv, pvB = Vlast[(j - 1) % 3][:, :], VlB[(j - 1) % 3]
                            else:
                                pv, pvB = None, None
                            mm(ps[pr, ob, c0:c0 + 128], Vcur[par][:, dg, h * 64:(h + 1) * 64], ShiftM,
                               False, s["last"] and pv is None, [VcB[par][dg], cstB], [psB[ob]], skip=True)
                            if pv is not None:
                                mm(ps[pr, ob, c0:c0 + 128], pv[:, h * 64:(h + 1) * 64], Dprev,
                                   False, s["last"], [pvB, cstB], [psB[ob]], skip=True)
                    if s["last"]:
                        tcopy("dve", oT[:, m, :], ps[:, ob, :], [psB[ob]], [oTB[m]])

                for it in range(-3, n):
                    if 0 <= it + 3 < n:
                        stZ(it + 3)
                    if 0 <= it + 2 < n:
                        stE(it + 2)
                    if 0 <= it + 1 < n:
                        stSP(it + 1)
                    if 0 <= it < n:
                        stW(it)
                        stAV(it)
                    if 0 <= it + 1 < n:
                        stC(it + 1)
                        stS(it + 1)
                    pull(gnext, -(-52 // n))

            def outnorm(j):
                par = j % 2
                blk = seq * NQB + j
                tt("pool", sq[:, :, :], oT[:, :, :], oT[:, :, :], ALU.mult, oTB, [sqB])
                b = PB
                for m in range(4):
                    mm(ps[:, b, :], ones, sq[:, m, :], m == 0, m == 3, [sqB, cstB], [psB[b]])
                rstd_big(ps[:, b, :], psB[b], rstd[:, :], rstdB, 1.0 / 512)
                for m in range(4):
                    stt("dve", mixo[par][:, m, :], oT[:, m, :], gpk[:, 11 + m:12 + m], rstd[:, :], ALU.mult, ALU.mult,
                        [oTB[m], rstdB, gpkB], [mixoB[par]])
                dma(mixs[blk, :, 0:4, :], mixo[par][:, :, :], ("mix", par), [mixoB[par]], [])

            drain(prep(0))
            for j in range(NQB):
                gnext = prep(j + 1) if j + 1 < NQB else None
                run_steps(j, gnext)
                outnorm(j)
                drain(gnext)

        def phase_mla(seq):
            A = Arena(W_END)
            knT = A.take([128, 4, S], BF16)
            krT = A.take([64, S], BF16)
            Vm = A.take([128, NQB * 4, 512], BF16)
            fr = front_alloc(A)
            uT = fr["uT"]
            cT = A.take([128, 3, 512], F32)
            sq = A.take([128, 4, 512], BF16)
            rstd = [A.take([128, 512], F32) for _ in range(2)]
            cn = A.take([128, 3, 512], BF16)
            ang = A.take([64, 512], F32)
            posi = ang.bitcast(I32)
            a2 = A.take([64, 512], F32)
            kf = A.take([64, 512], F32)
            ki = kf.bitcast(I32)
            tab = [A.take([64, 512], F32) for _ in range(2)]
            ro = A.take([64, 512], F32)
            rs_ = A.take([64, 512], F32)
            qnT = [A.take([128, 4, 512], BF16) for _ in range(2)]
            qrT = [A.take([64, 4, 512], BF16) for _ in range(2)]
            Pb = [A.take([128, 512], BF16) for _ in range(3)]
            Pacc = [A.take([128, 512], F32) for _ in range(2)]
            Pbf = A.take([128, 512], BF16)
            omT = A.take([128, 4, 512], F32)
            mixo = A.take([128, 4, 512], BF16)
            knTB = [T.bufs_n(f"knT{h}_", NQB) for h in range(4)]
            krTB = T.bufs_n("krT", NQB)
            VmB = T.bufs_n("Vm", NQB * 4)
            cTB = T.bufs_n("cT", 3)
            sqB = T.buf("sq")
            rstdB = T.bufs_n("rstd", 2)
            cnB = T.bufs_n("cn", 3)
            angB, a2B, kfB = (T.buf(nm) for nm in ("ang", "a2", "kf"))
            tabB = T.bufs_n("tab", 2)
            roB, rsB_ = T.buf("ro"), T.buf("rs_")
            qnTB = [T.bufs_n(f"qnT{p}_", 4) for p in range(2)]
            qrTB = [T.bufs_n(f"qrT{p}_", 4) for p in range(2)]
            PB_ = T.bufs_n("P", 3)
            PaccB = T.bufs_n("Pacc", 2)
            PbfB = T.buf("Pbf")
            omTB = T.bufs_n("omT", 4)
            mixoB = T.buf("mixo")
            chain_ctr = [0]
            nbk = lambda: next_bank(2)
            invf = gpk[0:64, 27:28]
            sin_t, cos_t = tab[0], tab[1]

            def rope(out_ap, outB, o_ap, oB, s_ap, sB):
                tt("dve", ro[:, :], o_ap, cos_t[:, :], ALU.mult, [oB, tabB[1]], [roB])
                tt("dve", rs_[:, :], s_ap, sin_t[:, :], ALU.mult, [sB, tabB[0]], [rsB_])
                tt("dve", out_ap, ro[:, :], rs_[:, :], ALU.add, [roB, rsB_], [outB])

            def prep(j):
                par = j % 2
                t0 = seq * S + j * 512
                yield from frontend(fr, t0, 0, nbk)
                dma(posi[:, :], pos[seq:seq + 1, j * 512:(j + 1) * 512].partition_broadcast(64), ("pos", 0), [], [angB])
                tcopy("dve", ang[:, :], posi[:, :], [], [angB])
                ts("dve", ang[:, :], ang[:, :], invf, None, ALU.mult, None, [gpkB], [angB])
                for ti, shift in enumerate((0.0, math.pi / 2)):
                    ts("dve", a2[:, :], ang[:, :], shift, None, ALU.add, None, [angB], [a2B])
                    ts("dve", kf[:, :], a2[:, :], 1.0 / TWO_PI, None, ALU.mult, None, [a2B], [kfB])
                    tcopy("dve", ki[:, :], kf[:, :], [], [kfB])
                    tcopy("dve", kf[:, :], ki[:, :], [], [kfB])
                    stt("dve", a2[:, :], kf[:, :], -C1, a2[:, :], ALU.mult, ALU.add, [kfB], [a2B])
                    stt("dve", a2[:, :], kf[:, :], -C2, a2[:, :], ALU.mult, ALU.add, [kfB], [a2B])
                    ts("dve", a2[:, :], a2[:, :], -3.141592, 3.141592, ALU.max, ALU.min, [], [a2B])
                    act(tab[ti][:, :], a2[:, :], AF.Sin, reads=[a2B], writes=[tabB[ti]])
                    yield
                for idx, col0 in enumerate((1536, 1664, 1792)):
                    b = nbk()
                    yield from chain8g(ps[:, b, :], b, lambda k: wall[:, k, col0:col0 + 128], lambda k: uT[:, k, :], [wallB, uTB])
                    tcopy("dve", cT[:, idx, :], ps[:, b, :], [psB[b]], [cTB[idx]])
                    yield
                tt("pool", sq[:, 0:3, :], cT[:, :, :], cT[:, :, :], ALU.mult, cTB, [sqB])
                bq = nbk()
                mm(ps[:, bq, :], ones, sq[:, 0, :], True, False, [sqB, cstB], [psB[bq]])
                mm(ps[:, bq, :], ones, sq[:, 1, :], False, True, [sqB, cstB], [psB[bq]])
                bk = nbk()
                mm(ps[:, bk, :], ones, sq[:, 2, :], True, True, [sqB, cstB], [psB[bk]])
                rstd_big(ps[:, bq, :], psB[bq], rstd[0][:, :], rstdB[0], 1.0 / 256)
                rstd_big(ps[:, bk, :], psB[bk], rstd[1][:, :], rstdB[1], 1.0 / 128)
                yield
                for idx in range(3):
                    r = 0 if idx < 2 else 1
                    stt("dve", cn[:, idx, :], cT[:, idx, :], gpk[:, 8 + idx:9 + idx], rstd[r][:, :], ALU.mult, ALU.mult,
                        [cTB[idx], rstdB[r], gpkB], [cnB[idx]])
                yield
                b1 = nbk()
                yield from chain8g(ps[0:64, b1, :], b1, lambda k: wall[:, k, 1920:1984], lambda k: uT[:, k, :], [wallB, uTB])
                b2 = nbk()
                yield from chain8g(ps[0:64, b2, :], b2, lambda k: wsw[:, k, :], lambda k: uT[:, k, :], [wswB, uTB])
                rope(krT[:, j * 512:(j + 1) * 512], krTB[j], ps[0:64, b1, :], psB[b1], ps[0:64, b2, :], psB[b2])
                yield
                for h in range(4):
                    b = nbk()
                    for k in range(2):
                        mm(ps[:, b, :], wqb[:, k, h * 192:h * 192 + 128], cn[:, k, :], k == 0, k == 1,
                           [wqbB, cnB[k]], [psB[b]])
                    tcopy("dve", qnT[par][:, h, :], ps[:, b, :], [psB[b]], [qnTB[par][h]])
                    b1 = nbk()
                    for k in range(2):
                        mm(ps[0:64, b1, :], wqb[:, k, h * 192 + 128:h * 192 + 192], cn[:, k, :], k == 0, k == 1,
                           [wqbB, cnB[k]], [psB[b1]])
                    b2 = nbk()
                    for k in range(2):
                        mm(ps[0:64, b2, :], wqbs[:, k, h * 64:(h + 1) * 64], cn[:, k, :], k == 0, k == 1,
                           [wqbsB, cnB[k]], [psB[b2]])
                    rope(qrT[par][:, h, :], qrTB[par][h], ps[0:64, b1, :], psB[b1], ps[0:64, b2, :], psB[b2])
                    yield
                for h in range(4):
                    b = nbk()
                    mm(ps[:, b, :], wkvb[:, h * 256:h * 256 + 128], cn[:, 2, :], True, True, [wkvbB, cnB[2]], [psB[b]])
                    tcopy("dve", knT[:, h, j * 512:(j + 1) * 512], ps[:, b, :], [psB[b]], [knTB[h][j]])
                    yield
                for c in range(4):
                    b = nbk()
                    mm(ps[:, b, :], cn[:, 2, c * 128:(c + 1) * 128], wkvv[:, :], True, True, [wkvvB, cnB[2]], [psB[b]])
                    tcopy("dve", Vm[:, j * 4 + c, :], ps[:, b, :], [psB[b]], [VmB[j * 4 + c]])
                    yield

            def run_steps(j, gnext):
                par = j % 2
                steps = []
                for h in range(4):
                    ch = chain_ctr[0]
                    chain_ctr[0] += 1
                    for kt in range(4 * j + 3, -1, -1):
                        dg = kt - 4 * j
                        steps.append(dict(h=h, kt=kt, dg=dg, c0=128 * dg if dg >= 0 else 0,
                                          first=(kt == 4 * j + 3), last=(kt == 0), ch=ch))
                n = len(steps)
                sc = 192.0 ** -0.5

                def mZ(i):
                    s = steps[i]
                    zb = 4 + (i % 2)
                    h, kt, c0 = s["h"], s["kt"], s["c0"]
                    mm(ps[:, zb, c0:512], knT[:, h, kt * 128:(kt + 1) * 128], qnT[par][:, h, c0:512], True, False,
                       [knTB[h][kt // 4], qnTB[par][h]], [psB[zb]])
                    mm(ps[:, zb, c0:512], krT[:, kt * 128:(kt + 1) * 128], qrT[par][:, h, c0:512], False, s["dg"] < 0,
                       [krTB[kt // 4], qrTB[par][h]], [psB[zb]])
                    if s["dg"] >= 0:
                        mm(ps[:, zb, c0:c0 + 128], ident, maskML, False, True, [cstB], [psB[zb]])

                def mP(i):
                    s = steps[i]
                    zb = 4 + (i % 2)
                    c0 = s["c0"]
                    act(Pb[i % 3][:, c0:512], ps[:, zb, c0:512], AF.Exp, reads=[psB[zb]], writes=[PB_[i % 3]], scale=sc)

                def mAV(i):
                    s = steps[i]
                    h, kt, c0 = s["h"], s["kt"], s["c0"]
                    ob = 6 + (s["ch"] % 2)
                    mm(ps[:, ob, c0:512], Vm[:, kt, h * 128:(h + 1) * 128], Pb[i % 3][:, c0:512], s["first"], s["last"],
                       [VmB[kt], PB_[i % 3]], [psB[ob]], skip=True)
                    db = 2 + (s["ch"] % 2)
                    mm(ps[:, db, c0:512], ones, Pb[i % 3][:, c0:512], s["first"], s["last"], [PB_[i % 3], cstB], [psB[db]], skip=True)
                    if s["last"]:
                        T.op("dve", lambda e: e.reciprocal(out=Pacc[1][:, :], in_=ps[:, db, :]), reads=[psB[db]],
                             writes=[PaccB[1]])
                        tt("dve", omT[:, h, :], ps[:, ob, :], Pacc[1][:, :], ALU.mult, [psB[ob], PaccB[1]], [omTB[h]])

                for it in range(-2, n):
                    if 0 <= it + 2 < n:
                        mZ(it + 2)
                    if 0 <= it + 1 < n:
                        mP(it + 1)
                    if 0 <= it < n:
                        mAV(it)
                    pull(gnext, -(-62 // n))

            def outnorm(j):
                blk = seq * NQB + j
                tt("pool", sq[:, :, :], omT[:, :, :], omT[:, :, :], ALU.mult, omTB, [sqB])
                b = nbk()
                for m in range(4):
                    mm(ps[:, b, :], ones, sq[:, m, :], m == 0, m == 3, [sqB, cstB], [psB[b]])
                rstd_big(ps[:, b, :], psB[b], rstd[0][:, :], rstdB[0], 1.0 / 512)
                for m in range(4):
                    stt("dve", mixo[:, m, :], omT[:, m, :], gpk[:, 15 + m:16 + m], rstd[0][:, :], ALU.mult, ALU.mult,
                        [omTB[m], rstdB[0], gpkB], [mixoB])
                dma(mixs[blk, :, 4:8, :], mixo[:, :, :], ("mix", 0), [mixoB], [])

            drain(prep(0))
            for j in range(NQB):
                gnext = prep(j + 1) if j + 1 < NQB else None
                run_steps(j, gnext)
                drain(gnext)
                outnorm(j)

        def phase_mlp():
            A = Arena(0)
            wo = A.take([128, 8, 1024], BF16)
            wup = A.take([128, 8, 4096], BF16)
            wdn = A.take([128, 32, 1024], BF16)
            gfin = A.take([128, 1024], F32)
            mixT = A.take([128, 8, 256], BF16)
            h1 = [A.take([128, 2, 1024], F32) for _ in range(2)]
            vtok = A.take([128, 1024], BF16)
            vT = [A.take([128, 8, 256], BF16) for _ in range(2)]
            hidT = A.take([128, 32, 256], BF16)
            hrall = A.take([128, 1024], BF16)
            hr = [hrall[:, q * 512:(q + 1) * 512].rearrange("p (a b) -> p a b", a=2) for q in range(2)]
            woB, wupB, wdnB, gfinB = T.buf("wo"), T.buf("wup"), T.buf("wdn"), T.buf("gfin")
            mixTB = T.buf("mixT")
            h1B = [T.bufs_n(f"h1_{p}_", 2) for p in range(2)]
            vtokB = T.buf("vtok")
            vTB = T.bufs_n("vT", 2)
            hidTB = T.bufs_n("hidT", 16)
            hrB = T.bufs_n("hr", 2)
            sB = [T.bufs_n(f"ms{q}_", 4) for q in range(3)]

            dma(gfin[:, :], gfin_d[0:1, :].partition_broadcast(128), ("c", 2), [], [gfinB])
            stg = [(h1[p][:, c, :], h1B[p][c]) for p in range(2) for c in range(2)]
            engs = ["dve", "pool", "act"]
            n = 0
            jobs = []
            for k in range(8):
                jobs.append((w_o[k * 128:(k + 1) * 128, :], wo[:, k, :], woB))
            for k in range(8):
                for q in range(4):
                    jobs.append((w_up[k * 128:(k + 1) * 128, q * 1024:(q + 1) * 1024], wup[:, k, q * 1024:(q + 1) * 1024], wupB))
            for f in range(32):
                jobs.append((w_dn[f * 128:(f + 1) * 128, :], wdn[:, f, :], wdnB))
            for src, dst, dB in jobs:
                sa, sBf = stg[n % 4]
                dma(sa, src, ("stg", n % 4), [], [sBf])
                eng = engs[n % 3]
                if eng == "act":
                    act(dst, sa, AF.Copy, reads=[sBf], writes=[dB])
                else:
                    tcopy(eng, dst, sa, [sBf], [dB])
                n += 1

            nb = NT // 256

            def stageA(bi):
                p = bi % 2
                t0 = bi * 256
                blk, half = bi // 2, bi % 2
                dma(mixT[:, :, :], mixs[blk, :, :, half * 256:(half + 1) * 256], ("mixl", 0), [], [mixTB])
                for c in range(2):
                    dma(h1[p][:, c, :], x[t0 + c * 128:t0 + (c + 1) * 128, :], ("xl", p * 2 + c), [], [h1B[p][c]])
                yield
                for c in range(2):
                    for hf in range(2):
                        b = next_bank(8)
                        for k in range(8):
                            mm(ps[:, b, :], mixT[:, k, c * 128:(c + 1) * 128], wo[:, k, hf * 512:(hf + 1) * 512],
                               k == 0, k == 7, [mixTB, woB], [psB[b]])
                        tt("dve", h1[p][:, c, hf * 512:(hf + 1) * 512], ps[:, b, :], h1[p][:, c, hf * 512:(hf + 1) * 512],
                           ALU.add, [psB[b]], [h1B[p][c]])
                        yield
                    act(vtok[:, :], h1[p][:, c, :], AF.Square, reads=[h1B[p][c]], writes=[vtokB, sB[0][c]],
                        accum=stat[:, c:c + 1])
                    rstd_small(stat[:, c:c + 1], stat[:, 4 + c:5 + c], stat[:, 8 + c:9 + c], 1.0 / 1024,
                               sB[0][c], sB[1][c], sB[2][c])
                    yield
                    ts("dve", vtok[:, :], h1[p][:, c, :], stat[:, 8 + c:9 + c], None, ALU.mult, None,
                       [h1B[p][c], sB[2][c]], [vtokB])
                    b = next_bank(8)
                    tpv = ps[:, b, :].bitcast(BF16)
                    for k in range(8):
                        T.op("pe", (lambda o, i: (lambda e: e.transpose(o, i, ident)))(tpv[:, k * 128:(k + 1) * 128],
                                                                                         vtok[:, k * 128:(k + 1) * 128]),
                             reads=[vtokB, cstB], writes=[psB[b]])
                    yield
                    for k in range(8):
                        ts("dve", vT[p][:, k, c * 128:(c + 1) * 128], tpv[:, k * 128:(k + 1) * 128],
                           gpk[:, 19 + k:20 + k], None, ALU.mult, None, [psB[b], gpkB], [vTB[p]])
                    yield

            def stageB(bi, gnext):
                p = bi % 2
                t0 = bi * 256
                for f2 in range(16):
                    b = next_bank(8)
                    for q in range(2):
                        f = 2 * f2 + q
                        for k in range(8):
                            mm(ps[:, b, q * 256:(q + 1) * 256], wup[:, k, f * 128:(f + 1) * 128], vT[p][:, k, :],
                               k == 0, k == 7, [wupB, vTB[p]], [psB[b]])
                    hq = f2 % 2
                    act(hr[hq][:, :, :], ps[:, b, :].rearrange("p (a b) -> p a b", a=2), AF.Relu, reads=[psB[b]],
                        writes=[hrB[hq]])
                    tt("pool", hidT[:, 2 * f2:2 * f2 + 2, :], hr[hq][:, :, :], hr[hq][:, :, :], ALU.mult,
                       [hrB[hq]], [hidTB[f2]])
                    if f2 % 2 == 1:
                        pull(gnext, 1)
                for c in range(2):
                    for hf in range(2):
                        b = next_bank(8)
                        for f in range(32):
                            mm(ps[:, b, :], hidT[:, f, c * 128:(c + 1) * 128], wdn[:, f, hf * 512:(hf + 1) * 512],
                               f == 0, f == 31, [hidTB[f // 2], wdnB], [psB[b]])
                        tt("dve", h1[p][:, c, hf * 512:(hf + 1) * 512], ps[:, b, :], h1[p][:, c, hf * 512:(hf + 1) * 512],
                           ALU.add, [psB[b]], [h1B[p][c]])
                        pull(gnext, 2)
                    act(hrall[:, :], h1[p][:, c, :], AF.Square, reads=[h1B[p][c]], writes=[hrB[0], hrB[1], sB[0][2 + c]],
                        accum=stat[:, 2 + c:3 + c])
                    rstd_small(stat[:, 2 + c:3 + c], stat[:, 6 + c:7 + c], stat[:, 10 + c:11 + c], 1.0 / 1024,
                               sB[0][2 + c], sB[1][2 + c], sB[2][2 + c])
                    stt("dve", h1[p][:, c, :], h1[p][:, c, :], stat[:, 10 + c:11 + c], gfin[:, :], ALU.mult, ALU.mult,
                        [sB[2][2 + c], gfinB], [h1B[p][c]])
                    dma(out[t0 + c * 128:t0 + (c + 1) * 128, :], h1[p][:, c, :], ("out", p * 2 + c), [h1B[p][c]], [])

            drain(stageA(0))
            for bi in range(nb):
                gnext = stageA(bi + 1) if bi + 1 < nb else None
                stageB(bi, gnext)
                drain(gnext)

        for seq in range(NSEQ):
            phase_sb(seq)
            T.barrier()
            phase_mla(seq)
            T.barrier()
        phase_mlp()
        T.emit(nc, es)
    return nc


def _consts():
    bf = ml_dtypes.bfloat16
    j = np.arange(128)[:, None]
    s = np.arange(128)[None, :]
    ident = (j == s).astype(np.float32)
    triU = (j >= s).astype(np.float32)
    ones = np.ones((128, 128), np.float32)
    maskSB = np.where(j >= s, NEG, 0.0)
    maskML = np.where(j > s, NEG, 0.0)
    Dmat = (j == s - 1).astype(np.float32) - (j == s).astype(np.float32)
    Dprev = ((j == 127) & (s == 0)).astype(np.float32)
    ShiftM = (j == s - 1).astype(np.float32)
    maskCp = -maskSB
    return np.concatenate([ident, triU, ones, maskSB, maskML, Dmat, Dprev, ShiftM, maskCp], 1).astype(bf)


def _gpk(attn_g, qa_g, kva_g, sb_g, mla_g, mlp_g):
    g = np.zeros((128, 32), np.float32)
    g[:, 0:8] = attn_g.reshape(8, 128).T
    g[:, 8:10] = qa_g.reshape(2, 128).T
    g[:, 10:11] = kva_g.reshape(1, 128).T
    g[:, 11:15] = sb_g.reshape(4, 128).T
    g[:, 15:19] = mla_g.reshape(4, 128).T
    g[:, 19:27] = mlp_g.reshape(8, 128).T
    half = 32
    inv = (10000.0 ** (-np.arange(half, dtype=np.float32) / half)).astype(np.float32)
    g[0:64, 27] = np.concatenate([inv, inv])
    return g


_NC_CACHE = {}


def run(x, positions, attn_norm_g, w_in, q_a_norm_g, w_q_b, kv_a_norm_g, w_kv_b, sb_out_norm_g, mla_out_norm_g,
        w_o, mlp_norm_g, w_up, w_down, final_norm_g, n_cores=N_CORES, dbg=False):
    x = np.asarray(x)
    B, S, Dm = x.shape
    nseq = B // n_cores
    nqb = S // 512
    key = (nseq, nqb, dbg)
    if key not in _NC_CACHE:
        _NC_CACHE[key] = build(nseq, nqb, dbg)
    nc = _NC_CACHE[key]
    f32 = lambda a: np.ascontiguousarray(np.asarray(a), dtype=np.float32)
    shared = {
        "w_in": f32(w_in)[0], "w_qb": f32(w_q_b)[0], "w_kvb": f32(w_kv_b)[0], "w_o": f32(w_o)[0],
        "w_up": f32(w_up)[0], "w_dn": f32(w_down)[0],
        "gpk": _gpk(f32(attn_norm_g)[0], f32(q_a_norm_g)[0], f32(kv_a_norm_g)[0], f32(sb_out_norm_g)[0],
                    f32(mla_out_norm_g)[0], f32(mlp_norm_g)[0]),
        "gfin": f32(final_norm_g).reshape(1, 1024),
        "cstb": _consts(),
    }
    positions = np.ascontiguousarray(np.asarray(positions), dtype=np.int32)
    in_maps = []
    for c in range(n_cores):
        m = dict(shared)
        m["x"] = np.ascontiguousarray(x[c * nseq:(c + 1) * nseq].reshape(nseq * S, Dm), dtype=np.float32)
        m["pos"] = np.ascontiguousarray(positions[c * nseq:(c + 1) * nseq])
        in_maps.append(m)
    res = run_bass_kernel_spmd(nc, in_maps, core_ids=list(range(n_cores)))
    outs = [np.asarray(r["out"]).reshape(nseq, S, Dm) for r in res.results]
    full = np.concatenate(outs, 0).astype(np.float32)
    if dbg:
        return full, [np.asarray(r["mixs"]) for r in res.results]
    return full


def kernel(**inputs):
    return run(**inputs)
```
